# Optimizing a Trainium2 kernel written in Bass

```python
import numpy as np
import jax, jax.numpy as jnp
from jax import lax

D_MODEL = 2048
BATCH = 8
SEQ = 2048
DEPTH = 2

D_CONV = D_MODEL // 4
CONV_WIDTH = 31
D_RNN = 3 * D_MODEL // 8
RNN_BLOCKS = 6
RNN_BLOCK_W = D_RNN // RNN_BLOCKS
RNN_CONV_WIDTH = 4
RG_C = 8.0
N_Q_HEADS = 6
N_KV_HEADS = 2
HEAD_DIM = 128
GROUP = N_Q_HEADS // N_KV_HEADS
D_ATTN = N_Q_HEADS * HEAD_DIM
KV_W = N_KV_HEADS * HEAD_DIM
CMP_BLOCK = 32
CMP_STRIDE = 16
SEL_BLOCK = 64
SEL_TOP_N = 16
WINDOW = 512
Q_BLOCK = 64
ROPE_THETA = 10000.0
D_FF = 4 * D_MODEL
NORM_EPS = 1e-6
NEG_INF = -1e30
POS_INF = 1e30

IN_SIZES = (D_CONV, D_CONV,
            D_RNN, D_RNN,
            D_ATTN,
            KV_W, KV_W, KV_W, KV_W, KV_W, KV_W,
            3 * N_Q_HEADS,
            D_MODEL, D_MODEL, D_MODEL)
N_IN = sum(IN_SIZES)

kernel_name = 'hybrid_conv_rglru_nsa_block'


def rms_norm(x, g):
    xf = x.astype(jnp.float32)
    y = xf * lax.rsqrt(jnp.mean(xf * xf, axis=-1, keepdims=True) + NORM_EPS)
    return (y * g).astype(x.dtype)


def layer_norm(x, g, b):
    xf = x.astype(jnp.float32)
    mu = jnp.mean(xf, axis=-1, keepdims=True)
    var = jnp.mean(jnp.square(xf - mu), axis=-1, keepdims=True)
    return ((xf - mu) * lax.rsqrt(var + NORM_EPS) * g + b).astype(x.dtype)


def masked_softmax(s, mask):
    p = jax.nn.softmax(jnp.where(mask, s, NEG_INF), axis=-1)
    return jnp.where(mask, p, 0.0)


def causal_depthwise_conv(x, w, b):
    k, c = w.shape
    y = lax.conv_general_dilated(x, w[:, None, :].astype(x.dtype), window_strides=(1,),
                                 padding=[(k - 1, 0)], dimension_numbers=('NWC', 'WIO', 'NWC'),
                                 feature_group_count=c)
    return y + b


def rope_tables(s):
    inv = 1.0 / (ROPE_THETA ** (jnp.arange(0, HEAD_DIM, 2, dtype=jnp.float32) / HEAD_DIM))
    ang = jnp.arange(s, dtype=jnp.float32)[:, None] * inv[None, :]
    return jnp.cos(ang)[:, None, :], jnp.sin(ang)[:, None, :]


def apply_rope(x, cos, sin):
    xf = x.astype(jnp.float32)
    x1, x2 = jnp.split(xf, 2, axis=-1)
    return jnp.concatenate([x1 * cos - x2 * sin, x2 * cos + x1 * sin], axis=-1).astype(x.dtype)


def _lin_combine(left, right):
    a1, b1 = left
    a2, b2 = right
    return a1 * a2, a2 * b1 + b2


def rg_lru(x, wa, ba, wx, bx, lam):
    b, s, d = x.shape
    xb = x.reshape(b, s, RNN_BLOCKS, RNN_BLOCK_W)
    r = jax.nn.sigmoid((jnp.einsum('bsnc,ncd->bsnd', xb, wa).reshape(b, s, d) + ba).astype(jnp.float32))
    i = jax.nn.sigmoid((jnp.einsum('bsnc,ncd->bsnd', xb, wx).reshape(b, s, d) + bx).astype(jnp.float32))
    log_a = -RG_C * jax.nn.softplus(-lam.astype(jnp.float32)) * r
    a = jnp.exp(log_a)
    gated_x = jnp.sqrt(-jnp.expm1(2.0 * log_a)) * (i * x.astype(jnp.float32))
    _, h = lax.associative_scan(_lin_combine, (a, gated_x), axis=1)
    return h.astype(x.dtype)


def nsa_attention(q, k_cmp, v_cmp, k_slc, v_slc, k_win, v_win, gates, cos, sin,
                  cmp_pe, cmp_k_w1, cmp_k_w2, cmp_v_w1, cmp_v_w2):
    b, s = q.shape[:2]
    scale = HEAD_DIM ** -0.5
    q_rot = apply_rope(q, cos, sin)
    k_slc = apply_rope(k_slc, cos, sin)
    k_win = apply_rope(k_win, cos, sin)

    n_cmp = (s - CMP_BLOCK) // CMP_STRIDE + 1
    cmp_idx = np.arange(n_cmp)[:, None] * CMP_STRIDE + np.arange(CMP_BLOCK)[None, :]

    def compress(kv, w1, w2):
        blk = kv[:, cmp_idx] + cmp_pe[None, None, :, None, :]
        blk = blk.transpose(0, 1, 3, 2, 4).reshape(b, n_cmp, N_KV_HEADS, CMP_BLOCK * HEAD_DIM)
        return jax.nn.gelu(blk @ w1) @ w2

    kc = compress(k_cmp, cmp_k_w1, cmp_k_w2)
    vc = compress(v_cmp, cmp_v_w1, cmp_v_w2)
    cmp_end = jnp.asarray(cmp_idx[:, -1], dtype=jnp.int32)

    n_sel = s // SEL_BLOCK
    n_top = min(SEL_TOP_N, n_sel)
    c_start = np.arange(n_cmp) * CMP_STRIDE
    s_start = np.arange(n_sel) * SEL_BLOCK
    overlap = jnp.asarray(((c_start[:, None] < s_start[None, :] + SEL_BLOCK)
                           & (c_start[:, None] + CMP_BLOCK > s_start[None, :])).astype(np.float32))
    ksb = k_slc.reshape(b, n_sel, SEL_BLOCK, N_KV_HEADS, HEAD_DIM).transpose(0, 3, 1, 2, 4)
    vsb = v_slc.reshape(b, n_sel, SEL_BLOCK, N_KV_HEADS, HEAD_DIM).transpose(0, 3, 1, 2, 4)
    gather = jax.vmap(jax.vmap(lambda blocks, idx: blocks[idx]))

    kw_pad = jnp.pad(k_win, ((0, 0), (WINDOW, 0), (0, 0), (0, 0)))
    vw_pad = jnp.pad(v_win, ((0, 0), (WINDOW, 0), (0, 0), (0, 0)))

    nq = s // Q_BLOCK

    def to_blocks(a):
        a = a.reshape(b, nq, Q_BLOCK, N_KV_HEADS, GROUP, a.shape[-1])
        return jnp.moveaxis(a, 1, 0)

    def block_fn(args):
        c, qn, qr, g = args
        t = c * Q_BLOCK + jnp.arange(Q_BLOCK, dtype=jnp.int32)
        s_c = jnp.einsum('bqhgd,bnhd->bhgqn', qn, kc).astype(jnp.float32) * scale
        p_c = masked_softmax(s_c, cmp_end[None, :] <= t[:, None])
        o_c = jnp.einsum('bhgqn,bnhd->bqhgd', p_c.astype(vc.dtype), vc)
        imp = jnp.einsum('bhgqn,nm->bhqm', p_c, overlap)
        blk = jnp.arange(n_sel, dtype=jnp.int32)[None, :]
        cur = (t // SEL_BLOCK)[:, None]
        valid = blk * SEL_BLOCK <= t[:, None]
        forced = (blk == 0) | (blk == cur) | (blk == cur - 1)
        score = jnp.where(valid, jnp.where(forced, POS_INF, imp), NEG_INF)
        _, idx = lax.top_k(score, n_top)
        kg = gather(ksb, idx)
        vg = gather(vsb, idx)
        kpos = idx[..., None] * SEL_BLOCK + jnp.arange(SEL_BLOCK, dtype=jnp.int32)
        m_s = (kpos <= t[:, None, None]).reshape(b, N_KV_HEADS, 1, Q_BLOCK, n_top * SEL_BLOCK)
        s_s = jnp.einsum('bqhgd,bhqnkd->bhgqnk', qr, kg).astype(jnp.float32) * scale
        p_s = masked_softmax(s_s.reshape(b, N_KV_HEADS, GROUP, Q_BLOCK, n_top * SEL_BLOCK), m_s)
        o_s = jnp.einsum('bhgqm,bhqmd->bqhgd', p_s.astype(vg.dtype),
                         vg.reshape(b, N_KV_HEADS, Q_BLOCK, n_top * SEL_BLOCK, HEAD_DIM))
        kw = lax.dynamic_slice_in_dim(kw_pad, c * Q_BLOCK, WINDOW + Q_BLOCK, axis=1)
        vw = lax.dynamic_slice_in_dim(vw_pad, c * Q_BLOCK, WINDOW + Q_BLOCK, axis=1)
        kpos_w = c * Q_BLOCK - WINDOW + jnp.arange(WINDOW + Q_BLOCK, dtype=jnp.int32)
        diff = t[:, None] - kpos_w[None, :]
        m_w = (diff >= 0) & (diff < WINDOW) & (kpos_w[None, :] >= 0)
        s_w = jnp.einsum('bqhgd,bkhd->bhgqk', qr, kw).astype(jnp.float32) * scale
        p_w = masked_softmax(s_w, m_w)
        o_w = jnp.einsum('bhgqk,bkhd->bqhgd', p_w.astype(vw.dtype), vw)
        return g[..., 0:1] * o_c + g[..., 1:2] * o_s + g[..., 2:3] * o_w

    o = lax.map(block_fn, (jnp.arange(nq, dtype=jnp.int32), to_blocks(q), to_blocks(q_rot), to_blocks(gates)))
    return jnp.moveaxis(o, 0, 1).reshape(b, s, D_ATTN)


def setup_inputs(seed: int = 0) -> dict:
    key = jax.random.key(seed)
    ks = jax.random.split(key, 32)
    L = DEPTH
    f32 = jnp.float32

    def nrm(k, shape, fan_in):
        return jax.random.normal(k, shape, f32) * (fan_in ** -0.5)

    def gain(k, shape):
        return 1.0 + 0.02 * jax.random.normal(k, shape, f32)

    def bias(k, shape):
        return 0.02 * jax.random.normal(k, shape, f32)

    u = jax.random.uniform(ks[14], (L, D_RNN), f32, minval=0.9, maxval=0.999)
    sa = u ** (1.0 / RG_C)
    return {
        'x': jax.random.normal(ks[0], (BATCH, SEQ, D_MODEL), f32),
        'attn_norm_g': gain(ks[1], (L, D_MODEL)),
        'w_in': nrm(ks[2], (L, D_MODEL, N_IN), D_MODEL),
        'conv_dw_w': nrm(ks[3], (L, CONV_WIDTH, D_CONV), CONV_WIDTH),
        'conv_dw_b': bias(ks[4], (L, D_CONV)),
        'conv_ln_g': gain(ks[5], (L, D_CONV)),
        'conv_ln_b': bias(ks[6], (L, D_CONV)),
        'w_conv_out': nrm(ks[7], (L, D_CONV, D_MODEL), D_CONV),
        'rnn_conv_w': nrm(ks[8], (L, RNN_CONV_WIDTH, D_RNN), RNN_CONV_WIDTH),
        'rnn_conv_b': bias(ks[9], (L, D_RNN)),
        'rglru_wa': nrm(ks[10], (L, RNN_BLOCKS, RNN_BLOCK_W, RNN_BLOCK_W), RNN_BLOCK_W),
        'rglru_ba': bias(ks[11], (L, D_RNN)),
        'rglru_wx': nrm(ks[12], (L, RNN_BLOCKS, RNN_BLOCK_W, RNN_BLOCK_W), RNN_BLOCK_W),
        'rglru_bx': bias(ks[13], (L, D_RNN)),
        'rglru_lambda': jnp.log(sa) - jnp.log1p(-sa),
        'w_rnn_out': nrm(ks[15], (L, D_RNN, D_MODEL), D_RNN),
        'cmp_pe': 0.02 * jax.random.normal(ks[16], (L, CMP_BLOCK, HEAD_DIM), f32),
        'cmp_k_w1': nrm(ks[17], (L, CMP_BLOCK * HEAD_DIM, HEAD_DIM), CMP_BLOCK * HEAD_DIM),
        'cmp_k_w2': nrm(ks[18], (L, HEAD_DIM, HEAD_DIM), HEAD_DIM),
        'cmp_v_w1': nrm(ks[19], (L, CMP_BLOCK * HEAD_DIM, HEAD_DIM), CMP_BLOCK * HEAD_DIM),
        'cmp_v_w2': nrm(ks[20], (L, HEAD_DIM, HEAD_DIM), HEAD_DIM),
        'w_attn_out': nrm(ks[21], (L, D_ATTN, D_MODEL), D_ATTN),
        'w_o': nrm(ks[22], (L, D_MODEL, D_MODEL), D_MODEL),
        'mlp_norm_g': gain(ks[23], (L, D_MODEL)),
        'w_mlp_up': nrm(ks[24], (L, D_MODEL, D_FF), D_MODEL),
        'w_mlp_down': nrm(ks[25], (L, D_FF, D_MODEL), D_FF),
        'final_norm_g': gain(ks[26], (D_MODEL,)),
    }


def reference(x, attn_norm_g, w_in, conv_dw_w, conv_dw_b, conv_ln_g, conv_ln_b, w_conv_out,
              rnn_conv_w, rnn_conv_b, rglru_wa, rglru_ba, rglru_wx, rglru_bx, rglru_lambda, w_rnn_out,
              cmp_pe, cmp_k_w1, cmp_k_w2, cmp_v_w1, cmp_v_w2, w_attn_out,
              w_o, mlp_norm_g, w_mlp_up, w_mlp_down, final_norm_g):
    b, s, _ = x.shape
    cos, sin = rope_tables(s)
    split_points = [int(v) for v in np.cumsum(IN_SIZES)[:-1]]

    def kv_heads(z):
        return z.reshape(b, s, N_KV_HEADS, HEAD_DIM)

    for l in range(DEPTH):
        h = rms_norm(x, attn_norm_g[l])
        proj = h @ w_in[l]
        (a_val, a_gate, r_x, r_gate, c_q, c_kc, c_vc, c_ks, c_vs, c_kw, c_vw, c_g,
         g_a, g_b, g_c) = jnp.split(proj, split_points, axis=-1)

        u = a_val * jax.nn.sigmoid(a_gate)
        u = causal_depthwise_conv(u, conv_dw_w[l], conv_dw_b[l])
        u = jax.nn.silu(layer_norm(u, conv_ln_g[l], conv_ln_b[l]))
        p_a = u @ w_conv_out[l]

        r = causal_depthwise_conv(r_x, rnn_conv_w[l], rnn_conv_b[l])
        r = rg_lru(r, rglru_wa[l], rglru_ba[l], rglru_wx[l], rglru_bx[l], rglru_lambda[l])
        p_b = (r * jax.nn.gelu(r_gate)) @ w_rnn_out[l]

        o = nsa_attention(c_q.reshape(b, s, N_Q_HEADS, HEAD_DIM), kv_heads(c_kc), kv_heads(c_vc),
                          kv_heads(c_ks), kv_heads(c_vs), kv_heads(c_kw), kv_heads(c_vw),
                          jax.nn.sigmoid(c_g).reshape(b, s, N_Q_HEADS, 3), cos, sin,
                          cmp_pe[l], cmp_k_w1[l], cmp_k_w2[l], cmp_v_w1[l], cmp_v_w2[l])
        p_c = o @ w_attn_out[l]

        y = jax.nn.sigmoid(g_a) * p_a + jax.nn.sigmoid(g_b) * p_b + jax.nn.sigmoid(g_c) * p_c
        x = x + y @ w_o[l]

        h2 = rms_norm(x, mlp_norm_g[l])
        x = x + jnp.square(jax.nn.relu(h2 @ w_mlp_up[l])) @ w_mlp_down[l]

    return rms_norm(x, final_norm_g)
```

```python
import numpy as np
import ml_dtypes
from contextlib import ExitStack
import concourse.bass as bass
import concourse.mybir as mybir
from concourse.bass_utils import run_bass_kernel_spmd

F32 = mybir.dt.float32
BF16 = mybir.dt.bfloat16
ALU = mybir.AluOpType
AF = mybir.ActivationFunctionType

D = 2048
S = 2048
DEPTH = 2
N_IN = 11026
OFF = dict(a_val=0, a_gate=512, r_x=1024, r_gate=1792, q=2560, kc=3328, vc=3584, ks=3840, vs=4096,
           kw=4352, vw=4608, cg=4864, ga=4882, gb=6930, gc=8978)
EPS = 1e-6
SCALE = 128 ** -0.5
NSLOT = 8
PREF = 4
MASKNEG = 30000.0

PP_L = dict(g_attn=16, g_mlp=16, convw=124, convb=4, lng=4, lnb=4, rcw=24, rcb=6, ba=6, bx=6, lam=6, peT=32)
PP_OFF = {}
_o = 0
for _l in range(DEPTH):
    for _k, _n in PP_L.items():
        PP_OFF[(_l, _k)] = _o
        _o += _n
PP_OFF["g_final"] = _o
_o += 16
NPP = _o
CF_OFF = dict(CT=0, ST=2048, mulM=4096, addM=4096 + 256, ones=4096 + 512)
NCF = 4096 + 512 + 128
CB_OFF = dict(maskC=0, Eall=2048, cm=4096, bm=4096 + 128, ident=4096 + 256, OV=4096 + 384)
NCB = 4096 + 384 + 32


def _col_tile(W, c0, ncol=128, nk=16):
    t = np.zeros((128, 16, 128), np.float32)
    for kc in range(nk):
        t[:, kc, :ncol] = W[kc * 128:(kc + 1) * 128, c0:c0 + ncol]
    return t


def _blk_tile(blocks):
    t = np.zeros((128, 16, 128), np.float32)
    for i, b in enumerate(blocks):
        t[:, i, :] = b
    return t


def _make_tile(inp, d):
    kind = d[0]
    if kind == "col":
        _, name, l, c0, ncol = d
        return _col_tile(inp[name][l], c0, ncol)
    if kind == "rglru":
        l = d[1]
        return _blk_tile([inp["rglru_wa"][l][n] for n in range(6)] + [inp["rglru_wx"][l][n] for n in range(6)])
    if kind == "w2":
        l = d[1]
        return _blk_tile([inp["cmp_k_w2"][l], inp["cmp_v_w2"][l]])
    if kind == "w1":
        _, nm, l, half = d
        w1 = inp[nm][l]
        return _blk_tile([w1[i * 128:(i + 1) * 128, :] for i in range(16 * half, 16 * half + 16)])
    if kind == "m3":
        _, l, j = d
        js = slice(j * 128, (j + 1) * 128)
        return _blk_tile([inp["w_conv_out"][l][c * 128:(c + 1) * 128, js] for c in range(4)]
                         + [inp["w_rnn_out"][l][c * 128:(c + 1) * 128, js] for c in range(6)]
                         + [inp["w_attn_out"][l][c * 128:(c + 1) * 128, js] for c in range(6)])
    if kind == "dn":
        _, l, qf, j = d
        dn = inp["w_mlp_down"][l]
        return _blk_tile([dn[(qf * 16 + f) * 128:(qf * 16 + f + 1) * 128, j * 128:(j + 1) * 128] for f in range(16)])
    raise ValueError(d)


def pack_weights(inp, worder):
    out = np.empty((len(worder), 128, 2048), np.float32)
    for i, d in enumerate(worder):
        out[i] = _make_tile(inp, d).reshape(128, 2048)
    return out


def _vec(v, nch):
    return np.asarray(v, np.float32).reshape(nch, 128).T


def pack_params(inp):
    pp = np.zeros((128, NPP), np.float32)

    def put(key, arr):
        o = PP_OFF[key]
        arr = np.asarray(arr, np.float32).reshape(128, -1)
        pp[:, o:o + arr.shape[1]] = arr

    for l in range(DEPTH):
        put((l, "g_attn"), _vec(inp["attn_norm_g"][l], 16))
        put((l, "g_mlp"), _vec(inp["mlp_norm_g"][l], 16))
        put((l, "convw"), np.asarray(inp["conv_dw_w"][l]).reshape(31, 4, 128).transpose(2, 1, 0))
        put((l, "convb"), _vec(inp["conv_dw_b"][l], 4))
        put((l, "lng"), _vec(inp["conv_ln_g"][l], 4))
        put((l, "lnb"), _vec(inp["conv_ln_b"][l], 4))
        put((l, "rcw"), np.asarray(inp["rnn_conv_w"][l]).reshape(4, 6, 128).transpose(2, 1, 0))
        put((l, "rcb"), _vec(inp["rnn_conv_b"][l], 6))
        put((l, "ba"), _vec(inp["rglru_ba"][l], 6))
        put((l, "bx"), _vec(inp["rglru_bx"][l], 6))
        put((l, "lam"), _vec(inp["rglru_lambda"][l], 6))
        put((l, "peT"), np.asarray(inp["cmp_pe"][l]).T)
    put("g_final", _vec(inp["final_norm_g"], 16))
    return pp


def make_consts():
    cf = np.zeros((128, NCF), np.float32)
    inv = (np.float32(1.0) / (np.float32(10000.0) ** (np.arange(0, 128, 2, dtype=np.float32) / np.float32(128)))).astype(np.float32)
    ang = (np.arange(S, dtype=np.float32)[:, None] * inv[None, :]).astype(np.float32)
    c = np.cos(ang).astype(np.float32).T
    s = np.sin(ang).astype(np.float32).T
    cf[0:64, 0:2048] = c
    cf[64:128, 0:2048] = c
    cf[0:64, 2048:4096] = -s
    cf[64:128, 2048:4096] = s
    m = np.arange(32)[None, :]
    for i in range(8, 16):
        t = (i * 128 + np.arange(128))[:, None]
        valid = m * 64 <= t
        cur = t // 64
        forced = (m == 0) | (m == cur) | (m == cur - 1)
        cf[:, CF_OFF["mulM"] + (i - 8) * 32: CF_OFF["mulM"] + (i - 7) * 32] = (valid & ~forced).astype(np.float32)
        add = np.where(valid, np.where(forced, 1e30, 0.0), -1e30)
        cf[:, CF_OFF["addM"] + (i - 8) * 32: CF_OFF["addM"] + (i - 7) * 32] = add
    cf[:, CF_OFF["ones"]:CF_OFF["ones"] + 128] = 1.0
    cb = np.zeros((128, NCB), np.float32)
    n = np.arange(128)[:, None]
    tt = np.arange(S)[None, :]
    cb[:, 0:2048] = ((16 * n + 31 <= tt) & (n < 127)).astype(np.float32)
    cb[0:32, 2048:4096] = (np.arange(32)[:, None] == (tt // 64)).astype(np.float32)
    jj = np.arange(128)[:, None]
    t2 = np.arange(128)[None, :]
    cb[:, 4096:4096 + 128] = (jj <= t2).astype(np.float32)
    cb[:, 4096 + 128:4096 + 256] = (jj > t2).astype(np.float32)
    cb[:, 4096 + 256:4096 + 384] = np.eye(128, dtype=np.float32)
    c_start = np.arange(127) * 16
    s_start = np.arange(32) * 64
    ov = ((c_start[:, None] < s_start[None, :] + 64) & (c_start[:, None] + 32 > s_start[None, :])).astype(np.float32)
    cb[0:127, 4096 + 384:4096 + 416] = ov
    return cf, cb.astype(ml_dtypes.bfloat16)


class Tok:
    __slots__ = ("sem", "val", "eng")

    def __init__(self, sem, val, eng):
        self.sem, self.val, self.eng = sem, val, eng


class Buf:
    __slots__ = ("name", "w", "r")

    def __init__(self, name=""):
        self.name = name
        self.w = None
        self.r = {}


class Eng:
    def __init__(self, e, sem, name, is_pe=False):
        self.e, self.sem, self.name, self.is_pe = e, sem, name, is_pe
        self.n = 0
        self.known = {}
        self.pending = []


class DSem:
    def __init__(self, sem):
        self.sem = sem
        self.n = 0


class _Stop(Exception):
    pass


class Builder:
    def __init__(self, nlayers=DEPTH, dbg=False, n_wtiles=None, stop=None):
        self.stop = stop
        self.nlayers = nlayers
        self.dbg = dbg
        self.nc = bass.Bass("TRN2", target_bir_lowering=False)
        self.es = ExitStack()
        self.all_dsems = []
        self.wi = 0
        self.wloaded = 0
        self.n_wtiles = n_wtiles
        self.worder = []

    def new_sem(self, name):
        return self.es.enter_context(self.nc.semaphore(name))

    def new_dsem(self, name):
        d = DSem(self.new_sem(name))
        self.all_dsems.append(d)
        return d

    def op(self, E, fn, R=(), W=(), ds=None, signal=True):
        deps = {}

        def need(tok):
            if tok is None:
                return
            if E.is_pe and tok.eng is E:
                return
            assert tok.val is not None, "unresolved pending token"
            k = id(tok.sem)
            if k not in deps or deps[k][1] < tok.val:
                deps[k] = (tok.sem, tok.val)

        for b in R:
            need(b.w)
        for b in W:
            need(b.w)
            for t in b.r.values():
                need(t)
        for k, (sem, val) in deps.items():
            if E.known.get(k, 0) < val:
                E.e.wait_ge(sem, val)
                E.known[k] = val
        ins = fn()
        if ds is not None:
            ds.n += 16
            ins.then_inc(ds.sem, 16)
            tok = Tok(ds.sem, ds.n, None)
        elif signal:
            E.n += 1
            ins.then_inc(E.sem, 1)
            tok = Tok(E.sem, E.n, E)
            for p in E.pending:
                p.val = E.n
            E.pending = []
        else:
            tok = Tok(E.sem, None, E)
            E.pending.append(tok)
        for b in R:
            b.r[id(tok.sem)] = tok
        for b in W:
            b.w = tok
            b.r = {}
        return tok

    def barrier(self):
        engs = [self.PE, self.ACT, self.DVE, self.SP]
        toks = []
        for E in [self.PE, self.ACT, self.DVE]:
            assert not E.pending
            if E.n:
                toks.append((E.sem, E.n))
        for d in self.sync_dsems:
            if d.n:
                toks.append((d.sem, d.n))
        for E in engs:
            for sem, val in toks:
                if E.known.get(id(sem), 0) < val:
                    E.e.wait_ge(sem, val)
                    E.known[id(sem)] = val
        self.bulk_i = 0

    def sb(self, es, name, shape, dt):
        self._sbn = getattr(self, "_sbn", 0) + 1
        return es.enter_context(self.nc.sbuf_tensor("sb%d_%s" % (self._sbn, name), shape, dt))

    def _wload(self):
        i = self.wloaded
        slot = i % NSLOT
        self.op(self.POOL, lambda: self.nc.gpsimd.dma_start(
            out=self.wring[:, slot, :, :].rearrange("p a b -> p (a b)"), in_=self.wpack[i], max_dma_last_dim=4096),
            R=(), W=[self.wbuf[slot]], ds=self.wds[slot])
        self.wloaded += 1

    def wnext(self, desc):
        self.worder.append(desc)
        i = self.wi
        while self.wloaded <= min(i + PREF, self.n_wtiles - 1):
            self._wload()
        self.wi += 1
        slot = i % NSLOT
        return self.wring[:, slot, :, :], self.wbuf[slot]

    def fstage(self):
        i = self.fst_i % len(self.fst)
        self.fst_i += 1
        return self.fst[i]

    def bstage(self):
        i = self.bst_i % len(self.bst)
        self.bst_i += 1
        return self.bst[i]

    def sdma(self, out, in_, R, W, ds):
        if ds is None:
            assert self.bulk_i < len(self.bulk), "bulk semaphore pool exhausted"
            ds = self.bulk[self.bulk_i]
            if self.bulk_i >= 6:
                pd = self.bulk[self.bulk_i - 6]
                if pd.n and self.SP.known.get(id(pd.sem), 0) < pd.n:
                    self.SP.e.wait_ge(pd.sem, pd.n)
                    self.SP.known[id(pd.sem)] = pd.n
            self.bulk_i += 1
        return self.op(self.SP, lambda: self.nc.sync.dma_start(out=out, in_=in_), R, W, ds=ds)

    def G(self, g):
        return self.ps[:, 4 * g:4 * g + 4, :].rearrange("p b n -> p (b n)")

    def Gb(self, g):
        return self.pb[4 * g:4 * g + 4]

    def mm(self, out, lhsT, rhs, start, stop, R, W, signal=None, sgc=False):
        nc = self.nc
        return self.op(self.PE, lambda: nc.tensor.matmul(out, lhsT, rhs, start=start, stop=stop,
                                                          skip_group_check=sgc),
                       R, W, signal=(stop if signal is None else signal))

    def proj_fm(self, wt, wb, nk, rhs_fn, rbufs, grp, koff=0):
        for tt in range(4):
            for kc in range(nk):
                self.mm(self.ps[:, grp * 4 + tt, :], wt[:, koff + kc, :], rhs_fn(kc, tt), kc == 0, kc == nk - 1,
                        R=[wb, rbufs[kc]], W=[self.pb[grp * 4 + tt]])

    def fill_setup(self, l):
        self.fill_jobs = [(p, j) for j in range(16) for p in range(3)]
        self.fill_l = l

    def fill(self, n):
        for _ in range(n):
            if not self.fill_jobs:
                return
            p, j = self.fill_jobs.pop(0)
            key = ("ga", "gb", "gc")[p]
            wt, wtb = self.wnext(("col", "w_in", self.fill_l, OFF[key] + j * 128, 128))
            g = self.fgrp
            self.fgrp ^= 1
            self.proj_fm(wt, wtb, 16, self.hrhs, self.Ab, g)
            bs, bsb, ds = self.bstage()
            self.act(bs[:, :], self.G(g), AF.Sigmoid, R=self.Gb(g), W=[bsb])
            self.sdma(self.sgs[p, j], bs[:, :], R=[bsb], W=[self.sgsb[p][j]], ds=ds)

    def act(self, out, in_, func, R, W, bias=None, scale=None):
        nc = self.nc
        kw = {}
        if bias is not None:
            kw["bias"] = bias
        if scale is not None:
            kw["scale"] = scale
        return self.op(self.ACT, lambda: nc.scalar.activation(out=out, in_=in_, func=func, **kw), R, W)

    def tt(self, out, in0, in1, op, R, W):
        nc = self.nc
        return self.op(self.DVE, lambda: nc.vector.tensor_tensor(out=out, in0=in0, in1=in1, op=op), R, W)

    def ts(self, out, in0, s1, s2, op0, op1, R, W):
        nc = self.nc
        if op1 is None:
            return self.op(self.DVE, lambda: nc.vector.tensor_scalar(out=out, in0=in0, scalar1=s1, scalar2=None,
                                                                     op0=op0), R, W)
        return self.op(self.DVE, lambda: nc.vector.tensor_scalar(out=out, in0=in0, scalar1=s1, scalar2=s2,
                                                                 op0=op0, op1=op1), R, W)

    def stt(self, out, in0, scalar, in1, op0, op1, R, W):
        nc = self.nc
        return self.op(self.DVE, lambda: nc.vector.scalar_tensor_tensor(out=out, in0=in0, scalar=scalar, in1=in1,
                                                                        op0=op0, op1=op1), R, W)

    def gelu(self, out, src, tmp, R, Wb, tmpb):
        self.act(tmp, src, AF.Square, R=R, W=[tmpb])
        self.ts(tmp, tmp, 0.044715, 1.0, ALU.mult, ALU.add, R=[tmpb], W=[tmpb])
        self.tt(tmp, tmp, src, ALU.mult, R=[tmpb] + list(R), W=[tmpb])
        self.act(tmp, tmp, AF.Sigmoid, R=[tmpb], W=[tmpb], scale=1.5957691216057308)
        self.tt(out, tmp, src, ALU.mult, R=[tmpb] + list(R), W=Wb)

    def build(self):
        nc = self.nc
        es = self.es
        L = self.nlayers
        dbg = self.dbg
        self.x_in = nc.dram_tensor("x_in", [16, 128, 2048], F32, kind="ExternalInput").ap()
        self.wpack = nc.dram_tensor("wpack", [self.n_wtiles, 128, 2048], F32, kind="ExternalInput").ap()
        pp_d = nc.dram_tensor("pp", [128, NPP], F32, kind="ExternalInput").ap()
        cf_d = nc.dram_tensor("cf", [128, NCF], F32, kind="ExternalInput").ap()
        cb_d = nc.dram_tensor("cb", [128, NCB], BF16, kind="ExternalInput").ap()
        self.out_d = nc.dram_tensor("out", [16, 128, 2048], F32, kind="ExternalOutput").ap()
        sk = "ExternalOutput" if dbg else "Internal"
        self.xs = nc.dram_tensor("xs", [16, 128, 2048], F32, kind=sk).ap()
        self.ys = nc.dram_tensor("ys", [16, 128, 2048], BF16, kind=sk).ap()
        self.us = nc.dram_tensor("us", [4, 128, 2048], BF16, kind=sk).ap()
        self.rbs = nc.dram_tensor("rbs", [6, 128, 2048], BF16, kind=sk).ap()
        self.os_ = nc.dram_tensor("os", [6, 128, 2048], BF16, kind=sk).ap()
        self.at = nc.dram_tensor("at", [2, 12, 128, 2048], BF16, kind="Internal").ap()
        self.sgs = nc.dram_tensor("sgs", [3, 16, 128, 2048], BF16, kind="Internal").ap()
        self.sgsb = [[Buf(f"sgs{p}_{j}") for j in range(16)] for p in range(3)]
        self.xinb = [Buf(f"xin{c}") for c in range(16)]
        self.xsb = [Buf(f"xs{c}") for c in range(16)]
        self.ysb = [Buf(f"ys{c}") for c in range(16)]
        self.usb = [Buf(f"us{c}") for c in range(4)]
        self.rbsb = [Buf(f"rbs{c}") for c in range(6)]
        self.osb = [Buf(f"os{c}") for c in range(6)]
        self.atb = [[Buf(f"at{h}_{i}") for i in range(12)] for h in range(2)]
        self.outb = [Buf(f"out{c}") for c in range(16)]

        self.PE = Eng(nc.tensor, self.new_sem("s_pe"), "pe", is_pe=True)
        self.ACT = Eng(nc.scalar, self.new_sem("s_act"), "act")
        self.DVE = Eng(nc.vector, self.new_sem("s_dve"), "dve")
        self.POOL = Eng(nc.gpsimd, self.new_sem("s_pool"), "pool")
        self.SP = Eng(nc.sync, self.new_sem("s_sp"), "sp")
        self.sync_dsems = []

        def sds(name):
            d = self.new_dsem(name)
            self.sync_dsems.append(d)
            return d

        self.A = self.sb(es, "A", [128, 16, 2048], BF16)
        self.Ab = [Buf(f"A{c}") for c in range(16)]
        self.wring = self.sb(es, "wring", [128, NSLOT, 16, 128], BF16)
        self.wbuf = [Buf(f"w{i}") for i in range(NSLOT)]
        self.wds = [self.new_dsem(f"wd{i}") for i in range(NSLOT)]
        self.fst = []
        for i in range(2):
            t = self.sb(es, f"fst{i}", [128, 2048], F32)
            self.fst.append((t, Buf(f"fst{i}"), sds(f"fsd{i}")))
        self.fst_i = 0
        self.bst = []
        for i in range(2):
            t = self.sb(es, f"bst{i}", [128, 2048], BF16)
            self.bst.append((t, Buf(f"bst{i}"), sds(f"bsd{i}")))
        self.bst_i = 0
        self.rstdB = self.sb(es, "rstdB", [128, 2048], F32)
        self.rstdb = Buf("rstdB")
        self.acc = self.sb(es, "acc", [128, 2048], F32)
        self.accb = Buf("acc")
        self.pp = self.sb(es, "pp", [128, NPP], F32)
        self.ppb = Buf("pp")
        self.ones_f = self.sb(es, "ones_f", [128, 128], F32)
        self.ident = self.sb(es, "ident", [128, 128], BF16)
        self.cstb = Buf("cst")
        self.gsb = self.sb(es, "gsb", [128, 16, 18], F32)
        self.gsbb = Buf("gsb")
        self.w2sb = self.sb(es, "w2sb", [128, 2, 128], BF16)
        self.w2sbb = Buf("w2sb")
        self.ps = es.enter_context(nc.psum_tensor("ps", [128, 8, 512], F32))
        self.pb = [Buf(f"ps{i}") for i in range(8)]
        self.cf_d, self.cb_d = cf_d, cb_d

        self.bulk = [sds(f"bulk{i}") for i in range(28)]
        self.bulk_i = 0
        self.sdma(self.pp[:, :], pp_d, R=(), W=[self.ppb], ds=None)
        self.sdma(self.ones_f[:, :], cf_d[:, CF_OFF["ones"]:CF_OFF["ones"] + 128], R=(), W=[self.cstb], ds=None)
        self.sdma(self.ident[:, :], cb_d[:, CB_OFF["ident"]:CB_OFF["ident"] + 128], R=(), W=[self.cstb], ds=None)
        self.p2vds = sds("p2vds")
        self.sgds = [[sds(f"sgd{k}_{p}") for p in range(3)] for k in range(2)]
        self.fill_jobs = []
        self.fgrp = 0

        with nc.allow_low_precision("bf16 matmul operands, fp32 accumulation"):
            try:
                self.phase0_stats()
                for l in range(L):
                    self.layer(l)
                self.final()
            except _Stop:
                pass
        self.barrier()
        for d in self.wds:
            if d.n and self.SP.known.get(id(d.sem), 0) < d.n:
                self.SP.e.wait_ge(d.sem, d.n)
        return nc

    def chk(self, l, name):
        self.barrier()
        if self.stop == (l, name):
            raise _Stop()

    def P(self, l, key, c0=0, n=1):
        o = PP_OFF[(l, key)] + c0
        return self.pp[:, o:o + n]

    def xsrc(self, l, c):
        if l == 0:
            return self.x_in[c], self.xinb[c]
        return self.xs[c], self.xsb[c]

    def acc_update(self, st, stb, sq, sqb, first):
        if first:
            self.tt(self.acc[:, :], st[:, :], st[:, :], ALU.mult, R=[stb], W=[self.accb])
        else:
            self.tt(sq[:, :], st[:, :], st[:, :], ALU.mult, R=[stb], W=[sqb])
            self.tt(self.acc[:, :], self.acc[:, :], sq[:, :], ALU.add, R=[sqb, self.accb], W=[self.accb])

    def phase0_stats(self):
        with ExitStack() as es:
            sq = self.sb(es, "p0sq", [128, 2048], F32)
            sqb = Buf("p0sq")
            for c in range(16):
                st, stb, ds = self.fstage()
                self.sdma(st[:, :], self.x_in[c], R=[self.xinb[c]], W=[stb], ds=ds)
                self.acc_update(st, stb, sq, sqb, c == 0)
            self.barrier()

    def norm_from_acc(self):
        for tt in range(4):
            self.mm(self.ps[:, tt, :], self.ones_f[:, :], self.acc[:, tt * 512:(tt + 1) * 512], True, True,
                    R=[self.accb, self.cstb], W=[self.pb[tt]])
        self.ts(self.rstdB[:, :], self.G(0), 1.0 / D, EPS, ALU.mult, ALU.add, R=self.Gb(0), W=[self.rstdb])
        self.act(self.rstdB[:, :], self.rstdB[:, :], AF.Sqrt, R=[self.rstdb], W=[self.rstdb])
        nc = self.nc
        self.op(self.DVE, lambda: nc.vector.reciprocal(out=self.rstdB[:, :], in_=self.rstdB[:, :]),
                R=[self.rstdb], W=[self.rstdb])

    def make_h(self, l, gkey):
        for c in range(16):
            st, stb, ds = self.fstage()
            src, srcb = self.xsrc(l, c) if gkey == "g_attn" else (self.xs[c], self.xsb[c])
            self.sdma(st[:, :], src, R=[srcb], W=[stb], ds=ds)
            self.stt(self.A[:, c, :], st[:, :], self.P(l, gkey, c), self.rstdB[:, :], ALU.mult, ALU.mult,
                     R=[stb, self.rstdb, self.ppb], W=[self.Ab[c]])

    def hrhs(self, kc, tt):
        return self.A[:, kc, tt * 512:(tt + 1) * 512]

    def layer(self, l):
        self.norm_from_acc()
        self.make_h(l, "g_attn")
        self.fill_setup(l)
        self.phase1(l)
        self.chk(l, "p1")
        self.phase2a(l)
        self.chk(l, "p2a")
        for hk in range(2):
            self.phase2b(l, hk)
            self.barrier()
        self.chk(l, "p2b")
        self.phase3(l)
        self.chk(l, "p3")
        self.phase4(l)
        self.chk(l, "p4")
        self.norm_from_acc()
        self.make_h(l, "g_mlp")
        self.phase6(l)
        self.chk(l, "p6")

    def phase1(self, l):
        nc = self.nc
        with ExitStack() as es:
            upad = self.sb(es, "upad", [128, 2080], F32)
            upb = Buf("upad")
            v = self.sb(es, "cv", [128, 4, 2048], F32)
            vb = [Buf(f"cv{c}") for c in range(4)]
            t1 = self.sb(es, "p1t1", [128, 2048], F32)
            t1b = Buf("p1t1")
            t2 = self.sb(es, "p1t2", [128, 2048], F32)
            t2b = Buf("p1t2")
            av = self.sb(es, "p1av", [128, 2048], F32)
            avb = Buf("p1av")
            self.op(self.DVE, lambda: nc.vector.memset(upad[:, 0:32], 0.0), R=(), W=[upb])
            for c in range(4):
                wv, wvb = self.wnext(("col", "w_in", l, OFF["a_val"] + c * 128, 128))
                self.proj_fm(wv, wvb, 16, self.hrhs, self.Ab, 0)
                self.act(av[:, :], self.G(0), AF.Copy, R=self.Gb(0), W=[avb])
                wg, wgb = self.wnext(("col", "w_in", l, OFF["a_gate"] + c * 128, 128))
                self.proj_fm(wg, wgb, 16, self.hrhs, self.Ab, 1)
                self.act(t1[:, :], self.G(1), AF.Sigmoid, R=self.Gb(1), W=[t1b])
                self.tt(upad[:, 30:2078], av[:, :], t1[:, :], ALU.mult, R=[avb, t1b], W=[upb])
                self.fill(3)
                cw = PP_OFF[(l, "convw")] + c * 31
                self.ts(v[:, c, :], upad[:, 0:2048], self.pp[:, cw:cw + 1], self.P(l, "convb", c), ALU.mult, ALU.add,
                        R=[upb, self.ppb], W=[vb[c]])
                for j in range(1, 31):
                    self.stt(v[:, c, :], upad[:, j:j + 2048], self.pp[:, cw + j:cw + j + 1], v[:, c, :],
                             ALU.mult, ALU.add, R=[upb, self.ppb, vb[c]], W=[vb[c]])
            for c in range(4):
                self.act(t1[:, :], v[:, c, :], AF.Square, R=[vb[c]], W=[t1b])
                for tt in range(4):
                    sl = slice(tt * 512, (tt + 1) * 512)
                    self.mm(self.ps[:, tt, :], self.ones_f[:, :], v[:, c, sl], c == 0, c == 3,
                            R=[vb[c], self.cstb], W=[self.pb[tt]])
                    self.mm(self.ps[:, 4 + tt, :], self.ones_f[:, :], t1[:, sl], c == 0, c == 3,
                            R=[t1b, self.cstb], W=[self.pb[4 + tt]], signal=True)
            self.ts(t1[:, :], self.G(0), 1.0 / 512, None, ALU.mult, None, R=self.Gb(0), W=[t1b])
            self.tt(t2[:, :], t1[:, :], t1[:, :], ALU.mult, R=[t1b], W=[t2b])
            self.stt(t2[:, :], self.G(1), 1.0 / 512, t2[:, :], ALU.mult, ALU.subtract, R=self.Gb(1) + [t2b], W=[t2b])
            self.ts(t2[:, :], t2[:, :], EPS, None, ALU.add, None, R=[t2b], W=[t2b])
            self.act(t2[:, :], t2[:, :], AF.Sqrt, R=[t2b], W=[t2b])
            self.op(self.DVE, lambda: nc.vector.reciprocal(out=t2[:, :], in_=t2[:, :]), R=[t2b], W=[t2b])
            for c in range(4):
                self.tt(v[:, c, :], v[:, c, :], t1[:, :], ALU.subtract, R=[vb[c], t1b], W=[vb[c]])
                self.tt(v[:, c, :], v[:, c, :], t2[:, :], ALU.mult, R=[vb[c], t2b], W=[vb[c]])
                bs, bsb, ds = self.bstage()
                self.act(bs[:, :], v[:, c, :], AF.Silu, R=[vb[c], self.ppb], W=[bsb],
                         bias=self.P(l, "lnb", c), scale=self.P(l, "lng", c))
                self.sdma(self.us[c], bs[:, :], R=[bsb], W=[self.usb[c]], ds=ds)
        self.barrier()
        with ExitStack() as es:
            xpad = self.sb(es, "xpad", [128, 2056], F32)
            xpb = Buf("xpad")
            gl = self.sb(es, "gl", [128, 2048], F32)
            glb = Buf("gl")
            r = self.sb(es, "rr", [128, 2048], F32)
            rb_ = Buf("rr")
            rbf = self.sb(es, "rbf", [128, 2048], BF16)
            rbfb = Buf("rbf")
            a = self.sb(es, "ra", [128, 2048], F32)
            ab = Buf("ra")
            ii = self.sb(es, "ri", [128, 2048], F32)
            iib = Buf("ri")
            tm = self.sb(es, "rtm", [128, 2048], F32)
            tmb = Buf("rtm")
            nl = self.sb(es, "nl", [128, 6], F32)
            nlb = Buf("nl")
            self.op(self.DVE, lambda: nc.vector.memset(xpad[:, 0:4], 0.0), R=(), W=[xpb])
            self.act(nl[:, :], self.P(l, "lam", 0, 6), AF.Exp, R=[self.ppb], W=[nlb], scale=-1.0)
            self.ts(nl[:, :], nl[:, :], 1.0, None, ALU.add, None, R=[nlb], W=[nlb])
            self.act(nl[:, :], nl[:, :], AF.Ln, R=[nlb], W=[nlb])
            self.ts(nl[:, :], nl[:, :], -8.0, None, ALU.mult, None, R=[nlb], W=[nlb])
            wr, wrb = self.wnext(("rglru", l))
            wrc = self.sb(es, "wrc", [128, 12, 128], BF16)
            wrcb = Buf("wrc")
            self.op(self.DVE, lambda: nc.vector.tensor_copy(out=wrc[:, :, :], in_=wr[:, 0:12, :]), R=[wrb], W=[wrcb])
            for c in range(6):
                wg, wgb = self.wnext(("col", "w_in", l, OFF["r_gate"] + c * 128, 128))
                self.proj_fm(wg, wgb, 16, self.hrhs, self.Ab, 0)
                self.gelu(gl[:, :], self.G(0), tm[:, :], R=self.Gb(0), Wb=[glb], tmpb=tmb)
                wx, wxb = self.wnext(("col", "w_in", l, OFF["r_x"] + c * 128, 128))
                self.proj_fm(wx, wxb, 16, self.hrhs, self.Ab, 1)
                self.act(xpad[:, 3:2051], self.G(1), AF.Copy, R=self.Gb(1), W=[xpb])
                self.fill(3)
                cw = PP_OFF[(l, "rcw")] + c * 4
                self.ts(r[:, :], xpad[:, 0:2048], self.pp[:, cw:cw + 1], self.P(l, "rcb", c), ALU.mult, ALU.add,
                        R=[xpb, self.ppb], W=[rb_])
                for j in range(1, 4):
                    self.stt(r[:, :], xpad[:, j:j + 2048], self.pp[:, cw + j:cw + j + 1], r[:, :], ALU.mult, ALU.add,
                             R=[xpb, self.ppb, rb_], W=[rb_])
                self.act(rbf[:, :], r[:, :], AF.Copy, R=[rb_], W=[rbfb])
                for tt in range(4):
                    sl = slice(tt * 512, (tt + 1) * 512)
                    self.mm(self.ps[:, tt, :], wrc[:, c, :], rbf[:, sl], True, True, R=[wrcb, rbfb], W=[self.pb[tt]])
                    self.mm(self.ps[:, 4 + tt, :], wrc[:, 6 + c, :], rbf[:, sl], True, True, R=[wrcb, rbfb],
                            W=[self.pb[4 + tt]])
                self.act(a[:, :], self.G(0), AF.Sigmoid, R=self.Gb(0) + [self.ppb], W=[ab], bias=self.P(l, "ba", c))
                self.act(a[:, :], a[:, :], AF.Exp, R=[ab, nlb], W=[ab], scale=nl[:, c:c + 1])
                self.act(ii[:, :], self.G(1), AF.Sigmoid, R=self.Gb(1) + [self.ppb], W=[iib], bias=self.P(l, "bx", c))
                self.tt(ii[:, :], ii[:, :], r[:, :], ALU.mult, R=[iib, rb_], W=[iib])
                self.tt(tm[:, :], a[:, :], a[:, :], ALU.mult, R=[ab], W=[tmb])
                self.ts(tm[:, :], tm[:, :], -1.0, 1.0, ALU.mult, ALU.add, R=[tmb], W=[tmb])
                self.act(tm[:, :], tm[:, :], AF.Sqrt, R=[tmb], W=[tmb])
                self.tt(ii[:, :], ii[:, :], tm[:, :], ALU.mult, R=[iib, tmb], W=[iib])
                self.op(self.DVE, lambda: nc.vector.tensor_tensor_scan(out=tm[:, :], data0=a[:, :], data1=ii[:, :],
                                                                       initial=0.0, op0=ALU.mult, op1=ALU.add),
                        R=[ab, iib], W=[tmb])
                bs, bsb, ds = self.bstage()
                self.tt(bs[:, :], tm[:, :], gl[:, :], ALU.mult, R=[tmb, glb], W=[bsb])
                self.sdma(self.rbs[c], bs[:, :], R=[bsb], W=[self.rbsb[c]], ds=ds)

    def phase2a(self, l):
        nc = self.nc
        with ExitStack() as es:
            CT = self.sb(es, "CT", [128, 2048], F32)
            ST = self.sb(es, "ST", [128, 2048], F32)
            ctb = Buf("CT")
            ta = self.sb(es, "p2ta", [128, 2048], F32)
            tab = Buf("p2ta")
            tb = self.sb(es, "p2tb", [128, 2048], F32)
            tbb = Buf("p2tb")
            vst = self.sb(es, "vst", [128, 16, 128], BF16)
            vstb = Buf("vst")
            vds = self.p2vds
            self.sdma(CT[:, :], self.cf_d[:, 0:2048], R=(), W=[ctb], ds=None)
            self.sdma(ST[:, :], self.cf_d[:, 2048:4096], R=(), W=[ctb], ds=None)
            wt, wtb = self.wnext(("col", "w_in", l, OFF["cg"], 18))
            first = True
            for i in range(16):
                for kc in range(16):
                    self.mm(self.ps[:, 0, i * 18:(i + 1) * 18], self.A[:, kc, i * 128:(i + 1) * 128], wt[:, kc, 0:18],
                            first, (i == 15 and kc == 15), R=[wtb, self.Ab[kc]], W=[self.pb[0]], sgc=True)
                    first = False
            self.act(self.gsb[:, :, :].rearrange("p a b -> p (a b)"), self.ps[:, 0, 0:288], AF.Sigmoid,
                     R=[self.pb[0]], W=[self.gsbb])
            grp = 1
            for hk in range(2):
                for idx in range(10):
                    if idx == 9:
                        continue
                    nm_ = ["q", "q", "q", "kc", "vc", "ks", "kw", "vs", "vw"][idx]
                    c0_ = OFF["q"] + (3 * hk + idx) * 128 if idx <= 2 else OFF[nm_] + hk * 128
                    wt, wtb = self.wnext(("col", "w_in", l, c0_, 128))
                    if idx <= 6:
                        self.proj_fm(wt, wtb, 16, self.hrhs, self.Ab, grp)
                        src = self.G(grp)
                        srcb = self.Gb(grp)
                        if idx <= 2 or idx in (3, 4):
                            bs, bsb, ds = self.bstage()
                            self.act(bs[:, :], src, AF.Copy, R=srcb, W=[bsb])
                            slot = idx if idx <= 2 else idx + 3
                            self.sdma(self.at[hk, slot], bs[:, :], R=[bsb], W=[self.atb[hk][slot]], ds=ds)
                        if idx <= 2 or idx in (5, 6):
                            self.act(tb[:, :], src, AF.Copy, R=srcb, W=[tbb])
                            self.op(self.DVE, lambda: nc.vector.tensor_copy(out=ta[0:64, :], in_=tb[64:128, :]),
                                    R=[tbb], W=[tab])
                            self.op(self.DVE, lambda: nc.vector.tensor_copy(out=ta[64:128, :], in_=tb[0:64, :]),
                                    R=[tbb, tab], W=[tab])
                            self.tt(ta[:, :], ta[:, :], ST[:, :], ALU.mult, R=[tab, ctb], W=[tab])
                            self.tt(tb[:, :], tb[:, :], CT[:, :], ALU.mult, R=[tbb, ctb], W=[tbb])
                            bs, bsb, ds = self.bstage()
                            self.tt(bs[:, :], tb[:, :], ta[:, :], ALU.add, R=[tab, tbb], W=[bsb])
                            slot = 3 + idx if idx <= 2 else idx + 3
                            self.sdma(self.at[hk, slot], bs[:, :], R=[bsb], W=[self.atb[hk][slot]], ds=ds)
                    else:
                        first = [True, True, True, True]
                        for i in range(16):
                            bnk = grp * 4 + i // 4
                            for kc in range(16):
                                self.mm(self.ps[:, bnk, (i % 4) * 128:(i % 4 + 1) * 128],
                                        self.A[:, kc, i * 128:(i + 1) * 128], wt[:, kc, :],
                                        first[i // 4], (i % 4 == 3 and kc == 15),
                                        R=[wtb, self.Ab[kc]], W=[self.pb[bnk]], sgc=True)
                                first[i // 4] = False
                        self.act(vst[:, :, :].rearrange("p a b -> p (a b)"), self.G(grp), AF.Copy, R=self.Gb(grp),
                                 W=[vstb])
                        slot = idx + 3
                        self.sdma(self.at[hk, slot], vst[:, :, :].rearrange("p a b -> p (a b)"), R=[vstb],
                                  W=[self.atb[hk][slot]], ds=vds)
                    grp ^= 1
                    self.fill(1)
            self.fill(100)

    def phase2b(self, l, hk):
        nc = self.nc
        with ExitStack() as es:
            A = self.A
            qT = A[:, 0:3, :]
            qrT = A[:, 3:6, :]
            kcmp = A[:, 6, :]
            vcmp = A[:, 7, :]
            ksT = A[:, 8, :]
            kwT = A[:, 9, :]
            oT = A[:, 12:15, :]
            lds = None
            inb = Buf("attn_in")
            for slot in range(10):
                self.sdma(A[:, slot, :], self.at[hk, slot], R=[self.atb[hk][slot]], W=[inb], ds=lds)
            vs = self.sb(es, "vs", [128, 16, 132], BF16)
            vw = self.sb(es, "vw", [128, 16, 132], BF16)
            vb_ = Buf("vsvw")
            self.op(self.DVE, lambda: nc.vector.memset(vs[:, :, 128:129], 1.0), R=(), W=[vb_])
            self.op(self.DVE, lambda: nc.vector.memset(vw[:, :, 128:129], 1.0), R=(), W=[vb_])
            self.sdma(vs[:, :, 0:128], self.at[hk, 10].rearrange("p (a b) -> p a b", b=128), R=[self.atb[hk][10]],
                      W=[vb_], ds=lds)
            self.sdma(vw[:, :, 0:128], self.at[hk, 11].rearrange("p (a b) -> p a b", b=128), R=[self.atb[hk][11]],
                      W=[vb_], ds=lds)
            maskC = self.sb(es, "maskC", [128, 2048], BF16)
            Eall = self.sb(es, "Eall", [128, 2048], BF16)
            cmk = self.sb(es, "cmk", [128, 2, 128], BF16)
            selM = self.sb(es, "selM", [128, 512], F32)
            cb2 = Buf("cb2")
            self.sdma(maskC[:, :], self.cb_d[:, 0:2048], R=(), W=[cb2], ds=lds)
            self.sdma(Eall[:, :], self.cb_d[:, 2048:4096], R=(), W=[cb2], ds=lds)
            self.sdma(cmk[:, :, :].rearrange("p a b -> p (a b)"), self.cb_d[:, 4096:4096 + 256], R=(), W=[cb2], ds=lds)
            self.sdma(selM[:, :], self.cf_d[:, CF_OFF["mulM"]:CF_OFF["mulM"] + 512], R=(), W=[cb2], ds=lds)
            vcx = self.sb(es, "vcx", [128, 164], BF16)
            vcxb = Buf("vcx")
            self.op(self.DVE, lambda: nc.vector.memset(vcx[:, :], 0.0), R=(), W=[vcxb])
            self.op(self.DVE, lambda: nc.vector.memset(vcx[:, 128:129], 1.0), R=(), W=[vcxb])
            self.sdma(vcx[:, 129:161], self.cb_d[:, CB_OFF["OV"]:CB_OFF["OV"] + 32], R=(), W=[vcxb], ds=lds)
            kcT = self.sb(es, "kcT", [128, 128], BF16)
            kcTb = Buf("kcT")
            slo = self.sb(es, "slo", [128, 2048], BF16)
            shi = self.sb(es, "shi", [128, 2048], BF16)
            slob = Buf("slohi")
            w2t, w2tb = self.wnext(("w2", l))
            self.op(self.DVE, lambda: nc.vector.tensor_copy(out=self.w2sb[:, :, :], in_=w2t[:, 0:2, :]), R=[w2tb],
                    W=[self.w2sbb])
            xk = self.sb(es, "xk", [128, 128], F32)
            xkb = Buf("xk")
            xt = self.sb(es, "xkt", [128, 128], F32)
            xtb = Buf("xkt")
            gT = self.sb(es, "gT", [128, 128], BF16)
            gTb = Buf("gT")
            for which in range(2):
                src = kcmp if which == 0 else vcmp
                srcv = src.rearrange("p (n r) -> p n r", r=16)
                lov = slo[:, :].rearrange("p (n r) -> p n r", r=16)
                hiv = shi[:, :].rearrange("p (n r) -> p n r", r=16)
                for rr in range(16):
                    self.ts(lov[:, :, rr], srcv[:, :, rr], self.P(l, "peT", rr), None, ALU.add, None,
                            R=[inb, self.ppb], W=[slob])
                    self.ts(hiv[:, :, rr], srcv[:, :, rr], self.P(l, "peT", 16 + rr), None, ALU.add, None,
                            R=[inb, self.ppb], W=[slob])
                w1n = "cmp_k_w1" if which == 0 else "cmp_v_w1"
                w1a, w1ab = self.wnext(("w1", w1n, l, 0))
                w1b, w1bb = self.wnext(("w1", w1n, l, 1))
                for li in range(32):
                    wt_, wtb_ = (w1a, w1ab) if li < 16 else (w1b, w1bb)
                    n0, rr = li // 16, li % 16
                    sv = lov if li < 16 else hiv
                    self.mm(self.ps[:, 0, 0:127], wt_[:, li % 16, :], sv[:, n0:n0 + 127, rr], li == 0, li == 31,
                            R=[wtb_, slob], W=[self.pb[0]])
                self.act(xk[:, 0:127], self.ps[:, 0, 0:127], AF.Copy, R=[self.pb[0]], W=[xkb])
                self.gelu(gT[:, 0:127], xk[:, 0:127], xt[:, 0:127], R=[xkb], Wb=[gTb], tmpb=xtb)
                if which == 0:
                    self.mm(self.ps[:, 2, 0:127], self.w2sb[:, 0, :], gT[:, 0:127], True, True, R=[self.w2sbb, gTb],
                            W=[self.pb[2]])
                    self.act(kcT[:, 0:127], self.ps[:, 2, 0:127], AF.Copy, R=[self.pb[2]], W=[kcTb])
                else:
                    self.mm(self.ps[0:127, 2, 0:128], gT[:, 0:127], self.w2sb[:, 1, :], True, True,
                            R=[self.w2sbb, gTb], W=[self.pb[2]])
                    self.act(vcx[0:127, 0:128], self.ps[0:127, 2, 0:128], AF.Copy, R=[self.pb[2]], W=[vcxb])
            esb = [self.sb(es, f"esb{i}", [128, 3, 128], BF16) for i in range(3)]
            esbb = [Buf(f"esb{i}") for i in range(3)]
            ectr = [0]
            den = self.sb(es, "den", [128, 3, 3], F32)
            denb = Buf("den")
            coef = self.sb(es, "coef", [128, 3, 3], F32)
            coefb = Buf("coef")
            imp = self.sb(es, "imp", [128, 32], F32)
            impb = Buf("imp")
            sc2 = self.sb(es, "sc2", [128, 32], F32)
            sc2b = Buf("sc2")
            m8 = self.sb(es, "m8", [128, 16], F32)
            m8b = Buf("m8")
            selb_ = self.sb(es, "selbf", [128, 128], BF16)
            selbb = Buf("selbf")
            self.op(self.DVE, lambda: nc.vector.memset(selb_[:, :], 0.0), R=(), W=[selbb])
            selneg = self.sb(es, "selneg", [128, 3, 128], BF16)
            selnegb = Buf("selneg")
            self.op(self.DVE, lambda: nc.vector.memset(selneg[:, :, :], 0.0), R=(), W=[selnegb])
            ofin = [self.sb(es, f"ofin{k}", [128, 3, 128], F32) for k in range(2)]
            ofinb = [Buf(f"ofin{k}") for k in range(2)]
            obf = [self.sb(es, f"obf{k}", [128, 3, 128], BF16) for k in range(2)]
            obfb = [Buf(f"obf{k}") for k in range(2)]
            oTb = Buf("oT")
            rd = self.sb(es, "rdc", [128, 3], F32)
            rdb = Buf("rdc")
            PS_S = [0, 1]
            PS_T, PS_OC = 2, 3
            PS_OS = [4, 5]
            PS_OW = [6, 7]
            sctr = [0]
            psb16 = self.ps[:, PS_T, :].bitcast(BF16)
            pend = [None]

            def flush():
                if pend[0] is not None:
                    f = pend[0]
                    pend[0] = None
                    f()

            def pair(i, kT_ap, nk, q_src, v_ap, vcols, ps_o, mk, use_sel, first, last):
                qsl = slice(i * 128, (i + 1) * 128)
                sb_ = PS_S[sctr[0] % 2]
                sctr[0] += 1
                self.mm(self.ps[0:nk, sb_, 0:384], kT_ap, q_src[:, :, qsl], True, not use_sel,
                        R=[inb, kcTb], W=[self.pb[sb_]], signal=not use_sel)
                if use_sel:
                    self.mm(self.ps[0:nk, sb_, 0:384], use_sel, selneg[:, :, :], False, True,
                            R=[cb2, selnegb], W=[self.pb[sb_]])
                ei = ectr[0] % 3
                ectr[0] += 1
                e, eb = esb[ei], esbb[ei]
                self.act(e[0:nk, :, :].rearrange("p a b -> p (a b)"), self.ps[0:nk, sb_, 0:384], AF.Exp,
                         R=[self.pb[sb_]], W=[eb], scale=SCALE)
                if mk is not None:
                    for g in range(3):
                        self.tt(e[0:nk, g, :], e[0:nk, g, :], mk, ALU.mult, R=[eb, cb2], W=[eb])
                flush()

                def pv():
                    for g in range(3):
                        lst = last and g == 2
                        self.mm(self.ps[:, ps_o, g * vcols:(g + 1) * vcols], e[0:nk, g, :], v_ap,
                                first and g == 0, lst, R=[eb, vb_, vcxb], W=[self.pb[ps_o]], signal=lst, sgc=True)
                pend[0] = pv

            deferred_T = [None]

            def run_deferred():
                if deferred_T[0] is not None:
                    f = deferred_T[0]
                    deferred_T[0] = None
                    f()

            for i in range(16):
                qsl = slice(i * 128, (i + 1) * 128)
                par = i % 2
                os_b, ow_b = PS_OS[par], PS_OW[par]
                sel_active = i >= 8
                pair(i, kcT[:, 0:127], 127, qT, vcx[0:127, 0:161], 161, PS_OC, maskC[0:127, qsl], None, True, True)
                wk = list(range(max(0, i - 4), i + 1))
                for ki, kt in enumerate(wk):
                    mk = cmk[:, 0, :] if kt == i else (cmk[:, 1, :] if kt == i - 4 else None)
                    pair(i, kwT[:, kt * 128:(kt + 1) * 128], 128, qrT, vw[:, kt, 0:129], 129, ow_b, mk, None,
                         ki == 0, ki == len(wk) - 1)
                    if ki == 0:
                        self.ts(den[:, :, 0], self.ps[:, PS_OC, 0:483].rearrange("p (g c) -> p g c", c=161)[:, :, 128],
                                1e-30, None, ALU.max, None, R=[self.pb[PS_OC]], W=[denb])
                        self.op(self.DVE, lambda: nc.vector.reciprocal(out=rd[:, :], in_=den[:, :, 0]), R=[denb],
                                W=[rdb])
                        self.tt(coef[:, :, 0], rd[:, :], self.gsb[:, i, hk * 9:hk * 9 + 9].rearrange(
                            "p (g c) -> p g c", c=3)[:, :, 0], ALU.mult, R=[rdb, self.gsbb], W=[coefb])
                        for g in range(3):
                            self.ts(ofin[par][:, g, :], self.ps[:, PS_OC, g * 161:g * 161 + 128], coef[:, g, 0:1], None,
                                    ALU.mult, None, R=[self.pb[PS_OC], coefb], W=[ofinb[par]])
                        if sel_active:
                            for g in range(3):
                                src_ = self.ps[:, PS_OC, g * 161 + 129:g * 161 + 161]
                                if g == 0:
                                    self.ts(imp[:, :], src_, rd[:, 0:1], None, ALU.mult, None,
                                            R=[self.pb[PS_OC], rdb], W=[impb])
                                else:
                                    self.stt(imp[:, :], src_, rd[:, g:g + 1], imp[:, :], ALU.mult, ALU.add,
                                             R=[self.pb[PS_OC], rdb, impb], W=[impb])
                            mo = (i - 8) * 32
                            self.tt(imp[:, :], imp[:, :], selM[:, mo:mo + 32], ALU.mult, R=[impb, cb2], W=[impb])
                            self.tt(imp[:, :], imp[:, :], selM[:, 256 + mo:256 + mo + 32], ALU.add, R=[impb, cb2],
                                    W=[impb])
                            self.op(self.DVE, lambda: nc.vector.max(out=m8[:, 0:8], in_=imp[:, :]), R=[impb], W=[m8b])
                            self.op(self.DVE, lambda: nc.vector.match_replace(out=sc2[:, :], in_to_replace=m8[:, 0:8],
                                                                              in_values=imp[:, :], imm_value=-3.0e38),
                                    R=[impb, m8b], W=[sc2b])
                            self.op(self.DVE, lambda: nc.vector.max(out=m8[:, 8:16], in_=sc2[:, :]), R=[sc2b, m8b],
                                    W=[m8b])
                            self.ts(sc2[:, :], imp[:, :], m8[:, 15:16], None, ALU.is_ge, None, R=[impb, m8b, sc2b],
                                    W=[sc2b])
                            self.ts(selb_[:, 0:32], sc2[:, :], -1.0, MASKNEG, ALU.add, ALU.mult, R=[sc2b], W=[selbb])
                    if ki == 1 or len(wk) == 1:
                        run_deferred()
                run_deferred()
                if sel_active:
                    self.op(self.PE, lambda: nc.tensor.transpose(psb16[:, 0:128], selb_[:, :], self.ident[:, :]),
                            R=[selbb, self.cstb], W=[self.pb[PS_T]])
                    for g in range(3):
                        self.op(self.DVE, lambda g=g: nc.vector.tensor_copy(out=selneg[:, g, :], in_=psb16[:, 0:128]),
                                R=[self.pb[PS_T]], W=[selnegb])
                for kt in range(i + 1):
                    pair(i, ksT[:, kt * 128:(kt + 1) * 128], 128, qrT, vs[:, kt, 0:129], 129, os_b,
                         cmk[:, 0, :] if kt == i else None,
                         Eall[:, kt * 128:(kt + 1) * 128] if sel_active else None, kt == 0, kt == i)
                flush()
                self.ts(den[:, :, 1], self.ps[:, os_b, 0:387].rearrange("p (g c) -> p g c", c=129)[:, :, 128],
                        1e-30, None, ALU.max, None, R=[self.pb[os_b]], W=[denb])
                self.ts(den[:, :, 2], self.ps[:, ow_b, 0:387].rearrange("p (g c) -> p g c", c=129)[:, :, 128],
                        1e-30, None, ALU.max, None, R=[self.pb[ow_b]], W=[denb])
                gv = self.gsb[:, i, hk * 9:hk * 9 + 9].rearrange("p (g c) -> p g c", c=3)
                self.op(self.DVE, lambda: nc.vector.reciprocal(out=coef[:, :, 1:3], in_=den[:, :, 1:3]),
                        R=[denb, coefb], W=[coefb])
                self.tt(coef[:, :, 1:3], coef[:, :, 1:3], gv[:, :, 1:3], ALU.mult, R=[coefb, self.gsbb], W=[coefb])
                for g in range(3):
                    self.stt(ofin[par][:, g, :], self.ps[:, os_b, g * 129:g * 129 + 128], coef[:, g, 1:2],
                             ofin[par][:, g, :], ALU.mult, ALU.add, R=[self.pb[os_b], coefb, ofinb[par]],
                             W=[ofinb[par]])
                    self.stt(obf[par][:, g, :], self.ps[:, ow_b, g * 129:g * 129 + 128], coef[:, g, 2:3],
                             ofin[par][:, g, :], ALU.mult, ALU.add, R=[self.pb[ow_b], coefb, ofinb[par]],
                             W=[obfb[par]])

                def do_T(par=par, qsl=qsl):
                    for g in range(3):
                        self.op(self.PE, lambda g=g: nc.tensor.transpose(psb16[:, 128 + g * 128:256 + g * 128],
                                                                         obf[par][:, g, :], self.ident[:, :]),
                                R=[obfb[par], self.cstb], W=[self.pb[PS_T]])
                    for g in range(3):
                        self.act(oT[:, g, qsl], psb16[:, 128 + g * 128:256 + g * 128], AF.Copy, R=[self.pb[PS_T]],
                                 W=[oTb])
                deferred_T[0] = do_T
            run_deferred()
            for g in range(3):
                self.sdma(self.os_[3 * hk + g], oT[:, g, :], R=[oTb], W=[self.osb[3 * hk + g]], ds=lds)

    def phase3(self, l):
        nc = self.nc
        with ExitStack() as es:
            A = self.A
            inb = Buf("p3in")
            for c in range(4):
                self.sdma(A[:, c, :], self.us[c], R=[self.usb[c]], W=[inb], ds=None)
            for c in range(6):
                self.sdma(A[:, 4 + c, :], self.rbs[c], R=[self.rbsb[c]], W=[inb], ds=None)
                self.sdma(A[:, 10 + c, :], self.os_[c], R=[self.osb[c]], W=[inb], ds=None)
            sgt = [self.sb(es, f"sgt{k}", [128, 3, 2048], BF16) for k in range(2)]
            sgtb = [[Buf(f"sgt{k}_{p}") for p in range(3)] for k in range(2)]
            yt = self.acc[:, 0:512]
            ytb = Buf("yt")
            tq = [self.acc[:, 512:1024], self.acc[:, 1024:1536]]
            tqb = [Buf("tq0"), Buf("tq1")]
            bctr = 0
            qctr = 0
            blocks = [(0, 4), (4, 6), (10, 6)]

            def load_sg(j):
                k = j % 2
                for p in range(3):
                    self.sdma(sgt[k][:, p, :], self.sgs[p, j], R=[self.sgsb[p][j]], W=[sgtb[k][p]], ds=self.sgds[k][p])

            load_sg(0)
            for j in range(16):
                if j + 1 < 16:
                    load_sg(j + 1)
                k = j % 2
                w3, w3b = self.wnext(("m3", l, j))
                bs, bsb, ds = self.bstage()
                for tt in range(4):
                    sl = slice(tt * 512, (tt + 1) * 512)
                    for p, (ko, nk) in enumerate(blocks):
                        bk = bctr % 8
                        bctr += 1
                        for kc in range(nk):
                            self.mm(self.ps[:, bk, :], w3[:, ko + kc, :], A[:, ko + kc, sl], kc == 0, kc == nk - 1,
                                    R=[w3b, inb], W=[self.pb[bk]])
                        if p == 0:
                            self.tt(yt, sgt[k][:, 0, sl], self.ps[:, bk, :], ALU.mult, R=[sgtb[k][0], self.pb[bk]],
                                    W=[ytb])
                        else:
                            q_, qb_ = tq[qctr % 2], tqb[qctr % 2]
                            qctr += 1
                            self.tt(q_, sgt[k][:, p, sl], self.ps[:, bk, :], ALU.mult, R=[sgtb[k][p], self.pb[bk]],
                                    W=[qb_])
                            if p == 1:
                                self.tt(yt, yt, q_, ALU.add, R=[qb_, ytb], W=[ytb])
                            else:
                                self.tt(bs[:, sl], yt, q_, ALU.add, R=[qb_, ytb], W=[bsb])
                self.sdma(self.ys[j], bs[:, :], R=[bsb], W=[self.ysb[j]], ds=ds)

    def resid_update(self, l, j, grp, sq, sqb, stats, first_stats, src_l0):
        st, stb, ds = self.fstage()
        if src_l0:
            src, srcb = self.xsrc(l, j)
        else:
            src, srcb = self.xs[j], self.xsb[j]
        self.sdma(st[:, :], src, R=[srcb], W=[stb], ds=ds)
        self.tt(st[:, :], st[:, :], self.G(grp), ALU.add, R=[stb] + self.Gb(grp), W=[stb])
        self.sdma(self.xs[j], st[:, :], R=[stb], W=[self.xsb[j]], ds=ds)
        if stats:
            self.acc_update(st, stb, sq, sqb, first_stats)

    def phase4(self, l):
        with ExitStack() as es:
            sq = self.sb(es, "p4sq", [128, 2048], F32)
            sqb = Buf("p4sq")
            for c in range(16):
                self.sdma(self.A[:, c, :], self.ys[c], R=[self.ysb[c]], W=[self.Ab[c]], ds=None)
            for j in range(16):
                wt, wtb = self.wnext(("col", "w_o", l, j * 128, 128))
                grp = j % 2
                self.proj_fm(wt, wtb, 16, self.hrhs, self.Ab, grp)
                self.resid_update(l, j, grp, sq, sqb, True, j == 0, True)

    def phase6(self, l):
        nc = self.nc
        with ExitStack() as es:
            actT = self.sb(es, "actT", [128, 16, 2048], BF16)
            actb = [Buf(f"act{f}") for f in range(16)]
            gctr = 0
            for qf in range(4):
                for f in range(16):
                    wt, wtb = self.wnext(("col", "w_mlp_up", l, (qf * 16 + f) * 128, 128))
                    grp = gctr % 2
                    gctr += 1
                    self.proj_fm(wt, wtb, 16, self.hrhs, self.Ab, grp)
                    st, stb, ds = self.fstage()
                    self.act(st[:, :], self.G(grp), AF.Square, R=self.Gb(grp), W=[stb])
                    self.stt(actT[:, f, :], self.G(grp), 0.0, st[:, :], ALU.is_gt, ALU.mult, R=self.Gb(grp) + [stb],
                             W=[actb[f]])
                for j in range(16):
                    wt, wtb = self.wnext(("dn", l, qf, j))
                    grp = gctr % 2
                    gctr += 1
                    self.proj_fm(wt, wtb, 16, lambda kc, tt: actT[:, kc, tt * 512:(tt + 1) * 512], actb, grp)
                    self.resid_update(l, j, grp, self.rstdB, self.rstdb, qf == 3, j == 0, False)

    def final(self):
        self.norm_from_acc()
        for c in range(16):
            st, stb, ds = self.fstage()
            self.sdma(st[:, :], self.xs[c], R=[self.xsb[c]], W=[stb], ds=ds)
            o = PP_OFF["g_final"] + c
            self.stt(st[:, :], st[:, :], self.pp[:, o:o + 1], self.rstdB[:, :], ALU.mult, ALU.mult,
                     R=[stb, self.rstdb, self.ppb], W=[stb])
            self.sdma(self.out_d[c], st[:, :], R=[stb], W=[self.outb[c]], ds=ds)


_CACHE = {}


N_WTILES = 516


def _get_builder():
    if "b" not in _CACHE:
        b = Builder(DEPTH, False, N_WTILES)
        b.build()
        assert len(b.worder) == N_WTILES, len(b.worder)
        _CACHE["b"] = b
    return _CACHE["b"]


def kernel(**inputs):
    inp = {k: np.asarray(v) for k, v in inputs.items()}
    bld = _get_builder()
    wpack = pack_weights(inp, bld.worder)
    pp = pack_params(inp)
    cf, cb = make_consts()
    x = inp["x"]
    nb = x.shape[0]
    in_maps = []
    for b in range(nb):
        xT = np.ascontiguousarray(x[b].T).reshape(16, 128, 2048)
        in_maps.append({"x_in": xT, "wpack": wpack, "pp": pp, "cf": cf, "cb": cb})
    nc = bld.nc
    res = run_bass_kernel_spmd(nc, in_maps, core_ids=list(range(nb)))
    outs = []
    for b in range(nb):
        o = np.asarray(res.results[b]["out"]).reshape(2048, 2048)
        outs.append(np.ascontiguousarray(o.T))
    return np.stack(outs, 0).astype(np.float32)
```

```python
import numpy as np
import ml_dtypes
from contextlib import ExitStack
import concourse.bass as bass
import concourse.mybir as mybir
from concourse.bass_utils import run_bass_kernel_spmd

F32 = mybir.dt.float32
BF16 = mybir.dt.bfloat16
ALU = mybir.AluOpType
AF = mybir.ActivationFunctionType

D = 2048
S = 2048
DEPTH = 2
N_IN = 11026
OFF = dict(a_val=0, a_gate=512, r_x=1024, r_gate=1792, q=2560, kc=3328, vc=3584, ks=3840, vs=4096,
           kw=4352, vw=4608, cg=4864, ga=4882, gb=6930, gc=8978)
EPS = 1e-6
SCALE = 128 ** -0.5
NSLOT = 8
PREF = 4
MASKNEG = 30000.0

PP_L = dict(g_attn=16, g_mlp=16, convw=124, convb=4, lng=4, lnb=4, rcw=24, rcb=6, ba=6, bx=6, lam=6, peT=32)
PP_OFF = {}
_o = 0
for _l in range(DEPTH):
    for _k, _n in PP_L.items():
        PP_OFF[(_l, _k)] = _o
        _o += _n
PP_OFF["g_final"] = _o
_o += 16
NPP = _o
CF_OFF = dict(CT=0, ST=2048, mulM=4096, addM=4096 + 256, ones=4096 + 512)
NCF = 4096 + 512 + 128
CB_OFF = dict(maskC=0, Eall=2048, cm=4096, bm=4096 + 128, ident=4096 + 256, OV=4096 + 384)
NCB = 4096 + 384 + 32


def _col_tile(W, c0, ncol=128, nk=16):
    t = np.zeros((128, 16, 128), np.float32)
    for kc in range(nk):
        t[:, kc, :ncol] = W[kc * 128:(kc + 1) * 128, c0:c0 + ncol]
    return t


def _blk_tile(blocks):
    t = np.zeros((128, 16, 128), np.float32)
    for i, b in enumerate(blocks):
        t[:, i, :] = b
    return t


def _make_tile(inp, d):
    kind = d[0]
    if kind == "col":
        _, name, l, c0, ncol = d
        return _col_tile(inp[name][l], c0, ncol)
    if kind == "rglru":
        l = d[1]
        return _blk_tile([inp["rglru_wa"][l][n] for n in range(6)] + [inp["rglru_wx"][l][n] for n in range(6)])
    if kind == "w2":
        l = d[1]
        return _blk_tile([inp["cmp_k_w2"][l], inp["cmp_v_w2"][l]])
    if kind == "w1":
        _, nm, l, half = d
        w1 = inp[nm][l]
        return _blk_tile([w1[i * 128:(i + 1) * 128, :] for i in range(16 * half, 16 * half + 16)])
    if kind == "m3":
        _, l, j = d
        js = slice(j * 128, (j + 1) * 128)
        return _blk_tile([inp["w_conv_out"][l][c * 128:(c + 1) * 128, js] for c in range(4)]
                         + [inp["w_rnn_out"][l][c * 128:(c + 1) * 128, js] for c in range(6)]
                         + [inp["w_attn_out"][l][c * 128:(c + 1) * 128, js] for c in range(6)])
    if kind == "dn":
        _, l, qf, j = d
        dn = inp["w_mlp_down"][l]
        return _blk_tile([dn[(qf * 16 + f) * 128:(qf * 16 + f + 1) * 128, j * 128:(j + 1) * 128] for f in range(16)])
    raise ValueError(d)


def pack_weights(inp, worder):
    out = np.empty((len(worder), 128, 2048), np.float32)
    for i, d in enumerate(worder):
        out[i] = _make_tile(inp, d).reshape(128, 2048)
    return out


def _vec(v, nch):
    return np.asarray(v, np.float32).reshape(nch, 128).T


def pack_params(inp):
    pp = np.zeros((128, NPP), np.float32)

    def put(key, arr):
        o = PP_OFF[key]
        arr = np.asarray(arr, np.float32).reshape(128, -1)
        pp[:, o:o + arr.shape[1]] = arr

    for l in range(DEPTH):
        put((l, "g_attn"), _vec(inp["attn_norm_g"][l], 16))
        put((l, "g_mlp"), _vec(inp["mlp_norm_g"][l], 16))
        put((l, "convw"), np.asarray(inp["conv_dw_w"][l]).reshape(31, 4, 128).transpose(2, 1, 0))
        put((l, "convb"), _vec(inp["conv_dw_b"][l], 4))
        put((l, "lng"), _vec(inp["conv_ln_g"][l], 4))
        put((l, "lnb"), _vec(inp["conv_ln_b"][l], 4))
        put((l, "rcw"), np.asarray(inp["rnn_conv_w"][l]).reshape(4, 6, 128).transpose(2, 1, 0))
        put((l, "rcb"), _vec(inp["rnn_conv_b"][l], 6))
        put((l, "ba"), _vec(inp["rglru_ba"][l], 6))
        put((l, "bx"), _vec(inp["rglru_bx"][l], 6))
        put((l, "lam"), _vec(inp["rglru_lambda"][l], 6))
        put((l, "peT"), np.asarray(inp["cmp_pe"][l]).T)
    put("g_final", _vec(inp["final_norm_g"], 16))
    return pp


def make_consts():
    cf = np.zeros((128, NCF), np.float32)
    inv = (np.float32(1.0) / (np.float32(10000.0) ** (np.arange(0, 128, 2, dtype=np.float32) / np.float32(128)))).astype(np.float32)
    ang = (np.arange(S, dtype=np.float32)[:, None] * inv[None, :]).astype(np.float32)
    c = np.cos(ang).astype(np.float32).T
    s = np.sin(ang).astype(np.float32).T
    cf[0:64, 0:2048] = c
    cf[64:128, 0:2048] = c
    cf[0:64, 2048:4096] = -s
    cf[64:128, 2048:4096] = s
    m = np.arange(32)[None, :]
    for i in range(8, 16):
        t = (i * 128 + np.arange(128))[:, None]
        valid = m * 64 <= t
        cur = t // 64
        forced = (m == 0) | (m == cur) | (m == cur - 1)
        cf[:, CF_OFF["mulM"] + (i - 8) * 32: CF_OFF["mulM"] + (i - 7) * 32] = (valid & ~forced).astype(np.float32)
        add = np.where(valid, np.where(forced, 1e30, 0.0), -1e30)
        cf[:, CF_OFF["addM"] + (i - 8) * 32: CF_OFF["addM"] + (i - 7) * 32] = add
    cf[:, CF_OFF["ones"]:CF_OFF["ones"] + 128] = 1.0
    cb = np.zeros((128, NCB), np.float32)
    n = np.arange(128)[:, None]
    tt = np.arange(S)[None, :]
    cb[:, 0:2048] = ((16 * n + 31 <= tt) & (n < 127)).astype(np.float32)
    cb[0:32, 2048:4096] = (np.arange(32)[:, None] == (tt // 64)).astype(np.float32)
    jj = np.arange(128)[:, None]
    t2 = np.arange(128)[None, :]
    cb[:, 4096:4096 + 128] = (jj <= t2).astype(np.float32)
    cb[:, 4096 + 128:4096 + 256] = (jj > t2).astype(np.float32)
    cb[:, 4096 + 256:4096 + 384] = np.eye(128, dtype=np.float32)
    c_start = np.arange(127) * 16
    s_start = np.arange(32) * 64
    ov = ((c_start[:, None] < s_start[None, :] + 64) & (c_start[:, None] + 32 > s_start[None, :])).astype(np.float32)
    cb[0:127, 4096 + 384:4096 + 416] = ov
    return cf, cb.astype(ml_dtypes.bfloat16)


class Tok:
    __slots__ = ("sem", "val", "eng")

    def __init__(self, sem, val, eng):
        self.sem, self.val, self.eng = sem, val, eng


class Buf:
    __slots__ = ("name", "w", "r")

    def __init__(self, name=""):
        self.name = name
        self.w = None
        self.r = {}


class Eng:
    def __init__(self, e, sem, name, is_pe=False):
        self.e, self.sem, self.name, self.is_pe = e, sem, name, is_pe
        self.n = 0
        self.known = {}
        self.pending = []


class DSem:
    def __init__(self, sem):
        self.sem = sem
        self.n = 0


class _Stop(Exception):
    pass


class Builder:
    def __init__(self, nlayers=DEPTH, dbg=False, n_wtiles=None, stop=None):
        self.stop = stop
        self.nlayers = nlayers
        self.dbg = dbg
        self.nc = bass.Bass("TRN2", target_bir_lowering=False)
        self.es = ExitStack()
        self.all_dsems = []
        self.wi = 0
        self.wloaded = 0
        self.n_wtiles = n_wtiles
        self.worder = []

    def new_sem(self, name):
        return self.es.enter_context(self.nc.semaphore(name))

    def new_dsem(self, name):
        d = DSem(self.new_sem(name))
        self.all_dsems.append(d)
        return d

    def op(self, E, fn, R=(), W=(), ds=None, signal=True):
        deps = {}

        def need(tok):
            if tok is None:
                return
            if E.is_pe and tok.eng is E:
                return
            assert tok.val is not None, "unresolved pending token"
            k = id(tok.sem)
            if k not in deps or deps[k][1] < tok.val:
                deps[k] = (tok.sem, tok.val)

        for b in R:
            need(b.w)
        for b in W:
            need(b.w)
            for t in b.r.values():
                need(t)
        for k, (sem, val) in deps.items():
            if E.known.get(k, 0) < val:
                E.e.wait_ge(sem, val)
                E.known[k] = val
        ins = fn()
        if ds is not None:
            ds.n += 16
            ins.then_inc(ds.sem, 16)
            tok = Tok(ds.sem, ds.n, None)
        elif signal:
            E.n += 1
            ins.then_inc(E.sem, 1)
            tok = Tok(E.sem, E.n, E)
            for p in E.pending:
                p.val = E.n
            E.pending = []
        else:
            tok = Tok(E.sem, None, E)
            E.pending.append(tok)
        for b in R:
            b.r[id(tok.sem)] = tok
        for b in W:
            b.w = tok
            b.r = {}
        return tok

    def barrier(self):
        engs = [self.PE, self.ACT, self.DVE, self.SP]
        toks = []
        for E in [self.PE, self.ACT, self.DVE]:
            assert not E.pending
            if E.n:
                toks.append((E.sem, E.n))
        for d in self.sync_dsems:
            if d.n:
                toks.append((d.sem, d.n))
        for E in engs:
            for sem, val in toks:
                if E.known.get(id(sem), 0) < val:
                    E.e.wait_ge(sem, val)
                    E.known[id(sem)] = val
        self.bulk_i = 0

    def sb(self, es, name, shape, dt):
        self._sbn = getattr(self, "_sbn", 0) + 1
        return es.enter_context(self.nc.sbuf_tensor("sb%d_%s" % (self._sbn, name), shape, dt))

    def _wload(self):
        i = self.wloaded
        slot = i % NSLOT
        self.op(self.POOL, lambda: self.nc.gpsimd.dma_start(
            out=self.wring[:, slot, :, :].rearrange("p a b -> p (a b)"), in_=self.wpack[i], max_dma_last_dim=4096),
            R=(), W=[self.wbuf[slot]], ds=self.wds[slot])
        self.wloaded += 1

    def wnext(self, desc):
        self.worder.append(desc)
        i = self.wi
        while self.wloaded <= min(i + PREF, self.n_wtiles - 1):
            self._wload()
        self.wi += 1
        slot = i % NSLOT
        return self.wring[:, slot, :, :], self.wbuf[slot]

    def fstage(self):
        i = self.fst_i % len(self.fst)
        self.fst_i += 1
        return self.fst[i]

    def bstage(self):
        i = self.bst_i % len(self.bst)
        self.bst_i += 1
        return self.bst[i]

    def sdma(self, out, in_, R, W, ds):
        if ds is None:
            assert self.bulk_i < len(self.bulk), "bulk semaphore pool exhausted"
            ds = self.bulk[self.bulk_i]
            if self.bulk_i >= 6:
                pd = self.bulk[self.bulk_i - 6]
                if pd.n and self.SP.known.get(id(pd.sem), 0) < pd.n:
                    self.SP.e.wait_ge(pd.sem, pd.n)
                    self.SP.known[id(pd.sem)] = pd.n
            self.bulk_i += 1
        return self.op(self.SP, lambda: self.nc.sync.dma_start(out=out, in_=in_), R, W, ds=ds)

    def G(self, g):
        return self.ps[:, 4 * g:4 * g + 4, :].rearrange("p b n -> p (b n)")

    def Gb(self, g):
        return self.pb[4 * g:4 * g + 4]

    def mm(self, out, lhsT, rhs, start, stop, R, W, signal=None, sgc=False):
        nc = self.nc
        return self.op(self.PE, lambda: nc.tensor.matmul(out, lhsT, rhs, start=start, stop=stop,
                                                          skip_group_check=sgc),
                       R, W, signal=(stop if signal is None else signal))

    def proj_fm(self, wt, wb, nk, rhs_fn, rbufs, grp, koff=0):
        for tt in range(4):
            for kc in range(nk):
                self.mm(self.ps[:, grp * 4 + tt, :], wt[:, koff + kc, :], rhs_fn(kc, tt), kc == 0, kc == nk - 1,
                        R=[wb, rbufs[kc]], W=[self.pb[grp * 4 + tt]])

    def fill_setup(self, l):
        self.fill_jobs = [(p, j) for j in range(16) for p in range(3)]
        self.fill_l = l

    def fill(self, n):
        for _ in range(n):
            if not self.fill_jobs:
                return
            p, j = self.fill_jobs.pop(0)
            key = ("ga", "gb", "gc")[p]
            wt, wtb = self.wnext(("col", "w_in", self.fill_l, OFF[key] + j * 128, 128))
            g = self.fgrp
            self.fgrp ^= 1
            self.proj_fm(wt, wtb, 16, self.hrhs, self.Ab, g)
            bs, bsb, ds = self.bstage()
            self.act(bs[:, :], self.G(g), AF.Sigmoid, R=self.Gb(g), W=[bsb])
            self.sdma(self.sgs[p, j], bs[:, :], R=[bsb], W=[self.sgsb[p][j]], ds=ds)

    def act(self, out, in_, func, R, W, bias=None, scale=None):
        nc = self.nc
        kw = {}
        if bias is not None:
            kw["bias"] = bias
        if scale is not None:
            kw["scale"] = scale
        return self.op(self.ACT, lambda: nc.scalar.activation(out=out, in_=in_, func=func, **kw), R, W)

    def tt(self, out, in0, in1, op, R, W):
        nc = self.nc
        return self.op(self.DVE, lambda: nc.vector.tensor_tensor(out=out, in0=in0, in1=in1, op=op), R, W)

    def ts(self, out, in0, s1, s2, op0, op1, R, W):
        nc = self.nc
        if op1 is None:
            return self.op(self.DVE, lambda: nc.vector.tensor_scalar(out=out, in0=in0, scalar1=s1, scalar2=None,
                                                                     op0=op0), R, W)
        return self.op(self.DVE, lambda: nc.vector.tensor_scalar(out=out, in0=in0, scalar1=s1, scalar2=s2,
                                                                 op0=op0, op1=op1), R, W)

    def stt(self, out, in0, scalar, in1, op0, op1, R, W):
        nc = self.nc
        return self.op(self.DVE, lambda: nc.vector.scalar_tensor_tensor(out=out, in0=in0, scalar=scalar, in1=in1,
                                                                        op0=op0, op1=op1), R, W)

    def gelu(self, out, src, tmp, R, Wb, tmpb):
        self.act(tmp, src, AF.Square, R=R, W=[tmpb])
        self.ts(tmp, tmp, 0.044715, 1.0, ALU.mult, ALU.add, R=[tmpb], W=[tmpb])
        self.tt(tmp, tmp, src, ALU.mult, R=[tmpb] + list(R), W=[tmpb])
        self.act(tmp, tmp, AF.Sigmoid, R=[tmpb], W=[tmpb], scale=1.5957691216057308)
        self.tt(out, tmp, src, ALU.mult, R=[tmpb] + list(R), W=Wb)

    def build(self):
        nc = self.nc
        es = self.es
        L = self.nlayers
        dbg = self.dbg
        self.x_in = nc.dram_tensor("x_in", [16, 128, 2048], F32, kind="ExternalInput").ap()
        self.wpack = nc.dram_tensor("wpack", [self.n_wtiles, 128, 2048], F32, kind="ExternalInput").ap()
        pp_d = nc.dram_tensor("pp", [128, NPP], F32, kind="ExternalInput").ap()
        cf_d = nc.dram_tensor("cf", [128, NCF], F32, kind="ExternalInput").ap()
        cb_d = nc.dram_tensor("cb", [128, NCB], BF16, kind="ExternalInput").ap()
        self.out_d = nc.dram_tensor("out", [16, 128, 2048], F32, kind="ExternalOutput").ap()
        sk = "ExternalOutput" if dbg else "Internal"
        self.xs = nc.dram_tensor("xs", [16, 128, 2048], F32, kind=sk).ap()
        self.ys = nc.dram_tensor("ys", [16, 128, 2048], BF16, kind=sk).ap()
        self.us = nc.dram_tensor("us", [4, 128, 2048], BF16, kind=sk).ap()
        self.rbs = nc.dram_tensor("rbs", [6, 128, 2048], BF16, kind=sk).ap()
        self.os_ = nc.dram_tensor("os", [6, 128, 2048], BF16, kind=sk).ap()
        self.at = nc.dram_tensor("at", [2, 12, 128, 2048], BF16, kind="Internal").ap()
        self.sgs = nc.dram_tensor("sgs", [3, 16, 128, 2048], BF16, kind="Internal").ap()
        self.sgsb = [[Buf(f"sgs{p}_{j}") for j in range(16)] for p in range(3)]
        self.xinb = [Buf(f"xin{c}") for c in range(16)]
        self.xsb = [Buf(f"xs{c}") for c in range(16)]
        self.ysb = [Buf(f"ys{c}") for c in range(16)]
        self.usb = [Buf(f"us{c}") for c in range(4)]
        self.rbsb = [Buf(f"rbs{c}") for c in range(6)]
        self.osb = [Buf(f"os{c}") for c in range(6)]
        self.atb = [[Buf(f"at{h}_{i}") for i in range(12)] for h in range(2)]
        self.outb = [Buf(f"out{c}") for c in range(16)]

        self.PE = Eng(nc.tensor, self.new_sem("s_pe"), "pe", is_pe=True)
        self.ACT = Eng(nc.scalar, self.new_sem("s_act"), "act")
        self.DVE = Eng(nc.vector, self.new_sem("s_dve"), "dve")
        self.POOL = Eng(nc.gpsimd, self.new_sem("s_pool"), "pool")
        self.SP = Eng(nc.sync, self.new_sem("s_sp"), "sp")
        self.sync_dsems = []

        def sds(name):
            d = self.new_dsem(name)
            self.sync_dsems.append(d)
            return d

        self.A = self.sb(es, "A", [128, 16, 2048], BF16)
        self.Ab = [Buf(f"A{c}") for c in range(16)]
        self.wring = self.sb(es, "wring", [128, NSLOT, 16, 128], BF16)
        self.wbuf = [Buf(f"w{i}") for i in range(NSLOT)]
        self.wds = [self.new_dsem(f"wd{i}") for i in range(NSLOT)]
        self.fst = []
        for i in range(2):
            t = self.sb(es, f"fst{i}", [128, 2048], F32)
            self.fst.append((t, Buf(f"fst{i}"), sds(f"fsd{i}")))
        self.fst_i = 0
        self.bst = []
        for i in range(2):
            t = self.sb(es, f"bst{i}", [128, 2048], BF16)
            self.bst.append((t, Buf(f"bst{i}"), sds(f"bsd{i}")))
        self.bst_i = 0
        self.rstdB = self.sb(es, "rstdB", [128, 2048], F32)
        self.rstdb = Buf("rstdB")
        self.acc = self.sb(es, "acc", [128, 2048], F32)
        self.accb = Buf("acc")
        self.pp = self.sb(es, "pp", [128, NPP], F32)
        self.ppb = Buf("pp")
        self.ones_f = self.sb(es, "ones_f", [128, 128], F32)
        self.ident = self.sb(es, "ident", [128, 128], BF16)
        self.cstb = Buf("cst")
        self.gsb = self.sb(es, "gsb", [128, 16, 18], F32)
        self.gsbb = Buf("gsb")
        self.w2sb = self.sb(es, "w2sb", [128, 2, 128], BF16)
        self.w2sbb = Buf("w2sb")
        self.ps = es.enter_context(nc.psum_tensor("ps", [128, 8, 512], F32))
        self.pb = [Buf(f"ps{i}") for i in range(8)]
        self.cf_d, self.cb_d = cf_d, cb_d

        self.bulk = [sds(f"bulk{i}") for i in range(28)]
        self.bulk_i = 0
        self.sdma(self.pp[:, :], pp_d, R=(), W=[self.ppb], ds=None)
        self.sdma(self.ones_f[:, :], cf_d[:, CF_OFF["ones"]:CF_OFF["ones"] + 128], R=(), W=[self.cstb], ds=None)
        self.sdma(self.ident[:, :], cb_d[:, CB_OFF["ident"]:CB_OFF["ident"] + 128], R=(), W=[self.cstb], ds=None)
        self.p2vds = sds("p2vds")
        self.sgds = [[sds(f"sgd{k}_{p}") for p in range(3)] for k in range(2)]
        self.fill_jobs = []
        self.fgrp = 0

        with nc.allow_low_precision("bf16 matmul operands, fp32 accumulation"):
            try:
                self.phase0_stats()
                for l in range(L):
                    self.layer(l)
                self.final()
            except _Stop:
                pass
        self.barrier()
        for d in self.wds:
            if d.n and self.SP.known.get(id(d.sem), 0) < d.n:
                self.SP.e.wait_ge(d.sem, d.n)
        return nc

    def chk(self, l, name):
        self.barrier()
        if self.stop == (l, name):
            raise _Stop()

    def P(self, l, key, c0=0, n=1):
        o = PP_OFF[(l, key)] + c0
        return self.pp[:, o:o + n]

    def xsrc(self, l, c):
        if l == 0:
            return self.x_in[c], self.xinb[c]
        return self.xs[c], self.xsb[c]

    def acc_update(self, st, stb, sq, sqb, first):
        if first:
            self.tt(self.acc[:, :], st[:, :], st[:, :], ALU.mult, R=[stb], W=[self.accb])
        else:
            self.tt(sq[:, :], st[:, :], st[:, :], ALU.mult, R=[stb], W=[sqb])
            self.tt(self.acc[:, :], self.acc[:, :], sq[:, :], ALU.add, R=[sqb, self.accb], W=[self.accb])

    def phase0_stats(self):
        with ExitStack() as es:
            sq = self.sb(es, "p0sq", [128, 2048], F32)
            sqb = Buf("p0sq")
            for c in range(16):
                st, stb, ds = self.fstage()
                self.sdma(st[:, :], self.x_in[c], R=[self.xinb[c]], W=[stb], ds=ds)
                self.acc_update(st, stb, sq, sqb, c == 0)
            self.barrier()

    def norm_from_acc(self):
        for tt in range(4):
            self.mm(self.ps[:, tt, :], self.ones_f[:, :], self.acc[:, tt * 512:(tt + 1) * 512], True, True,
                    R=[self.accb, self.cstb], W=[self.pb[tt]])
        self.ts(self.rstdB[:, :], self.G(0), 1.0 / D, EPS, ALU.mult, ALU.add, R=self.Gb(0), W=[self.rstdb])
        self.act(self.rstdB[:, :], self.rstdB[:, :], AF.Sqrt, R=[self.rstdb], W=[self.rstdb])
        nc = self.nc
        self.op(self.DVE, lambda: nc.vector.reciprocal(out=self.rstdB[:, :], in_=self.rstdB[:, :]),
                R=[self.rstdb], W=[self.rstdb])

    def make_h(self, l, gkey):
        for c in range(16):
            st, stb, ds = self.fstage()
            src, srcb = self.xsrc(l, c) if gkey == "g_attn" else (self.xs[c], self.xsb[c])
            self.sdma(st[:, :], src, R=[srcb], W=[stb], ds=ds)
            self.stt(self.A[:, c, :], st[:, :], self.P(l, gkey, c), self.rstdB[:, :], ALU.mult, ALU.mult,
                     R=[stb, self.rstdb, self.ppb], W=[self.Ab[c]])

    def hrhs(self, kc, tt):
        return self.A[:, kc, tt * 512:(tt + 1) * 512]

    def layer(self, l):
        self.norm_from_acc()
        self.make_h(l, "g_attn")
        self.fill_setup(l)
        self.phase1(l)
        self.chk(l, "p1")
        self.phase2a(l)
        self.chk(l, "p2a")
        for hk in range(2):
            self.phase2b(l, hk)
            self.barrier()
        self.chk(l, "p2b")
        self.phase3(l)
        self.chk(l, "p3")
        self.phase4(l)
        self.chk(l, "p4")
        self.norm_from_acc()
        self.make_h(l, "g_mlp")
        self.phase6(l)
        self.chk(l, "p6")

    def phase1(self, l):
        nc = self.nc
        with ExitStack() as es:
            upad = self.sb(es, "upad", [128, 2080], F32)
            upb = Buf("upad")
            v = self.sb(es, "cv", [128, 4, 2048], F32)
            vb = [Buf(f"cv{c}") for c in range(4)]
            t1 = self.sb(es, "p1t1", [128, 2048], F32)
            t1b = Buf("p1t1")
            t2 = self.sb(es, "p1t2", [128, 2048], F32)
            t2b = Buf("p1t2")
            av = self.sb(es, "p1av", [128, 2048], F32)
            avb = Buf("p1av")
            self.op(self.DVE, lambda: nc.vector.memset(upad[:, 0:32], 0.0), R=(), W=[upb])
            for c in range(4):
                wv, wvb = self.wnext(("col", "w_in", l, OFF["a_val"] + c * 128, 128))
                self.proj_fm(wv, wvb, 16, self.hrhs, self.Ab, 0)
                self.act(av[:, :], self.G(0), AF.Copy, R=self.Gb(0), W=[avb])
                wg, wgb = self.wnext(("col", "w_in", l, OFF["a_gate"] + c * 128, 128))
                self.proj_fm(wg, wgb, 16, self.hrhs, self.Ab, 1)
                self.act(t1[:, :], self.G(1), AF.Sigmoid, R=self.Gb(1), W=[t1b])
                self.tt(upad[:, 30:2078], av[:, :], t1[:, :], ALU.mult, R=[avb, t1b], W=[upb])
                self.fill(3)
                cw = PP_OFF[(l, "convw")] + c * 31
                self.ts(v[:, c, :], upad[:, 0:2048], self.pp[:, cw:cw + 1], self.P(l, "convb", c), ALU.mult, ALU.add,
                        R=[upb, self.ppb], W=[vb[c]])
                for j in range(1, 31):
                    self.stt(v[:, c, :], upad[:, j:j + 2048], self.pp[:, cw + j:cw + j + 1], v[:, c, :],
                             ALU.mult, ALU.add, R=[upb, self.ppb, vb[c]], W=[vb[c]])
            for c in range(4):
                self.act(t1[:, :], v[:, c, :], AF.Square, R=[vb[c]], W=[t1b])
                for tt in range(4):
                    sl = slice(tt * 512, (tt + 1) * 512)
                    self.mm(self.ps[:, tt, :], self.ones_f[:, :], v[:, c, sl], c == 0, c == 3,
                            R=[vb[c], self.cstb], W=[self.pb[tt]])
                    self.mm(self.ps[:, 4 + tt, :], self.ones_f[:, :], t1[:, sl], c == 0, c == 3,
                            R=[t1b, self.cstb], W=[self.pb[4 + tt]], signal=True)
            self.ts(t1[:, :], self.G(0), 1.0 / 512, None, ALU.mult, None, R=self.Gb(0), W=[t1b])
            self.tt(t2[:, :], t1[:, :], t1[:, :], ALU.mult, R=[t1b], W=[t2b])
            self.stt(t2[:, :], self.G(1), 1.0 / 512, t2[:, :], ALU.mult, ALU.subtract, R=self.Gb(1) + [t2b], W=[t2b])
            self.ts(t2[:, :], t2[:, :], EPS, None, ALU.add, None, R=[t2b], W=[t2b])
            self.act(t2[:, :], t2[:, :], AF.Sqrt, R=[t2b], W=[t2b])
            self.op(self.DVE, lambda: nc.vector.reciprocal(out=t2[:, :], in_=t2[:, :]), R=[t2b], W=[t2b])
            for c in range(4):
                self.tt(v[:, c, :], v[:, c, :], t1[:, :], ALU.subtract, R=[vb[c], t1b], W=[vb[c]])
                self.tt(v[:, c, :], v[:, c, :], t2[:, :], ALU.mult, R=[vb[c], t2b], W=[vb[c]])
                bs, bsb, ds = self.bstage()
                self.act(bs[:, :], v[:, c, :], AF.Silu, R=[vb[c], self.ppb], W=[bsb],
                         bias=self.P(l, "lnb", c), scale=self.P(l, "lng", c))
                self.sdma(self.us[c], bs[:, :], R=[bsb], W=[self.usb[c]], ds=ds)
        self.barrier()
        with ExitStack() as es:
            xpad = self.sb(es, "xpad", [128, 2056], F32)
            xpb = Buf("xpad")
            gl = self.sb(es, "gl", [128, 2048], F32)
            glb = Buf("gl")
            r = self.sb(es, "rr", [128, 2048], F32)
            rb_ = Buf("rr")
            rbf = self.sb(es, "rbf", [128, 2048], BF16)
            rbfb = Buf("rbf")
            a = self.sb(es, "ra", [128, 2048], F32)
            ab = Buf("ra")
            ii = self.sb(es, "ri", [128, 2048], F32)
            iib = Buf("ri")
            tm = self.sb(es, "rtm", [128, 2048], F32)
            tmb = Buf("rtm")
            nl = self.sb(es, "nl", [128, 6], F32)
            nlb = Buf("nl")
            self.op(self.DVE, lambda: nc.vector.memset(xpad[:, 0:4], 0.0), R=(), W=[xpb])
            self.act(nl[:, :], self.P(l, "lam", 0, 6), AF.Exp, R=[self.ppb], W=[nlb], scale=-1.0)
            self.ts(nl[:, :], nl[:, :], 1.0, None, ALU.add, None, R=[nlb], W=[nlb])
            self.act(nl[:, :], nl[:, :], AF.Ln, R=[nlb], W=[nlb])
            self.ts(nl[:, :], nl[:, :], -8.0, None, ALU.mult, None, R=[nlb], W=[nlb])
            wr, wrb = self.wnext(("rglru", l))
            wrc = self.sb(es, "wrc", [128, 12, 128], BF16)
            wrcb = Buf("wrc")
            self.op(self.DVE, lambda: nc.vector.tensor_copy(out=wrc[:, :, :], in_=wr[:, 0:12, :]), R=[wrb], W=[wrcb])
            for c in range(6):
                wg, wgb = self.wnext(("col", "w_in", l, OFF["r_gate"] + c * 128, 128))
                self.proj_fm(wg, wgb, 16, self.hrhs, self.Ab, 0)
                self.gelu(gl[:, :], self.G(0), tm[:, :], R=self.Gb(0), Wb=[glb], tmpb=tmb)
                wx, wxb = self.wnext(("col", "w_in", l, OFF["r_x"] + c * 128, 128))
                self.proj_fm(wx, wxb, 16, self.hrhs, self.Ab, 1)
                self.act(xpad[:, 3:2051], self.G(1), AF.Copy, R=self.Gb(1), W=[xpb])
                self.fill(3)
                cw = PP_OFF[(l, "rcw")] + c * 4
                self.ts(r[:, :], xpad[:, 0:2048], self.pp[:, cw:cw + 1], self.P(l, "rcb", c), ALU.mult, ALU.add,
                        R=[xpb, self.ppb], W=[rb_])
                for j in range(1, 4):
                    self.stt(r[:, :], xpad[:, j:j + 2048], self.pp[:, cw + j:cw + j + 1], r[:, :], ALU.mult, ALU.add,
                             R=[xpb, self.ppb, rb_], W=[rb_])
                self.act(rbf[:, :], r[:, :], AF.Copy, R=[rb_], W=[rbfb])
                for tt in range(4):
                    sl = slice(tt * 512, (tt + 1) * 512)
                    self.mm(self.ps[:, tt, :], wrc[:, c, :], rbf[:, sl], True, True, R=[wrcb, rbfb], W=[self.pb[tt]])
                    self.mm(self.ps[:, 4 + tt, :], wrc[:, 6 + c, :], rbf[:, sl], True, True, R=[wrcb, rbfb],
                            W=[self.pb[4 + tt]])
                self.act(a[:, :], self.G(0), AF.Sigmoid, R=self.Gb(0) + [self.ppb], W=[ab], bias=self.P(l, "ba", c))
                self.act(a[:, :], a[:, :], AF.Exp, R=[ab, nlb], W=[ab], scale=nl[:, c:c + 1])
                self.act(ii[:, :], self.G(1), AF.Sigmoid, R=self.Gb(1) + [self.ppb], W=[iib], bias=self.P(l, "bx", c))
                self.tt(ii[:, :], ii[:, :], r[:, :], ALU.mult, R=[iib, rb_], W=[iib])
                self.tt(tm[:, :], a[:, :], a[:, :], ALU.mult, R=[ab], W=[tmb])
                self.ts(tm[:, :], tm[:, :], -1.0, 1.0, ALU.mult, ALU.add, R=[tmb], W=[tmb])
                self.act(tm[:, :], tm[:, :], AF.Sqrt, R=[tmb], W=[tmb])
                self.tt(ii[:, :], ii[:, :], tm[:, :], ALU.mult, R=[iib, tmb], W=[iib])
                self.op(self.DVE, lambda: nc.vector.tensor_tensor_scan(out=tm[:, :], data0=a[:, :], data1=ii[:, :],
                                                                       initial=0.0, op0=ALU.mult, op1=ALU.add),
                        R=[ab, iib], W=[tmb])
                bs, bsb, ds = self.bstage()
                self.tt(bs[:, :], tm[:, :], gl[:, :], ALU.mult, R=[tmb, glb], W=[bsb])
                self.sdma(self.rbs[c], bs[:, :], R=[bsb], W=[self.rbsb[c]], ds=ds)

    def phase2a(self, l):
        nc = self.nc
        with ExitStack() as es:
            CT = self.sb(es, "CT", [128, 2048], F32)
            ST = self.sb(es, "ST", [128, 2048], F32)
            ctb = Buf("CT")
            ta = self.sb(es, "p2ta", [128, 2048], F32)
            tab = Buf("p2ta")
            tb = self.sb(es, "p2tb", [128, 2048], F32)
            tbb = Buf("p2tb")
            vst = self.sb(es, "vst", [128, 16, 128], BF16)
            vstb = Buf("vst")
            vds = self.p2vds
            self.sdma(CT[:, :], self.cf_d[:, 0:2048], R=(), W=[ctb], ds=None)
            self.sdma(ST[:, :], self.cf_d[:, 2048:4096], R=(), W=[ctb], ds=None)
            wt, wtb = self.wnext(("col", "w_in", l, OFF["cg"], 18))
            first = True
            for i in range(16):
                for kc in range(16):
                    self.mm(self.ps[:, 0, i * 18:(i + 1) * 18], self.A[:, kc, i * 128:(i + 1) * 128], wt[:, kc, 0:18],
                            first, (i == 15 and kc == 15), R=[wtb, self.Ab[kc]], W=[self.pb[0]], sgc=True)
                    first = False
            self.act(self.gsb[:, :, :].rearrange("p a b -> p (a b)"), self.ps[:, 0, 0:288], AF.Sigmoid,
                     R=[self.pb[0]], W=[self.gsbb])
            grp = 1
            for hk in range(2):
                for idx in range(10):
                    if idx == 9:
                        continue
                    nm_ = ["q", "q", "q", "kc", "vc", "ks", "kw", "vs", "vw"][idx]
                    c0_ = OFF["q"] + (3 * hk + idx) * 128 if idx <= 2 else OFF[nm_] + hk * 128
                    wt, wtb = self.wnext(("col", "w_in", l, c0_, 128))
                    if idx <= 6:
                        self.proj_fm(wt, wtb, 16, self.hrhs, self.Ab, grp)
                        src = self.G(grp)
                        srcb = self.Gb(grp)
                        if idx <= 2 or idx in (3, 4):
                            bs, bsb, ds = self.bstage()
                            self.act(bs[:, :], src, AF.Copy, R=srcb, W=[bsb])
                            slot = idx if idx <= 2 else idx + 3
                            self.sdma(self.at[hk, slot], bs[:, :], R=[bsb], W=[self.atb[hk][slot]], ds=ds)
                        if idx <= 2 or idx in (5, 6):
                            self.act(tb[:, :], src, AF.Copy, R=srcb, W=[tbb])
                            self.op(self.DVE, lambda: nc.vector.tensor_copy(out=ta[0:64, :], in_=tb[64:128, :]),
                                    R=[tbb], W=[tab])
                            self.op(self.DVE, lambda: nc.vector.tensor_copy(out=ta[64:128, :], in_=tb[0:64, :]),
                                    R=[tbb, tab], W=[tab])
                            self.tt(ta[:, :], ta[:, :], ST[:, :], ALU.mult, R=[tab, ctb], W=[tab])
                            self.tt(tb[:, :], tb[:, :], CT[:, :], ALU.mult, R=[tbb, ctb], W=[tbb])
                            bs, bsb, ds = self.bstage()
                            self.tt(bs[:, :], tb[:, :], ta[:, :], ALU.add, R=[tab, tbb], W=[bsb])
                            slot = 3 + idx if idx <= 2 else idx + 3
                            self.sdma(self.at[hk, slot], bs[:, :], R=[bsb], W=[self.atb[hk][slot]], ds=ds)
                    else:
                        first = [True, True, True, True]
                        for i in range(16):
                            bnk = grp * 4 + i // 4
                            for kc in range(16):
                                self.mm(self.ps[:, bnk, (i % 4) * 128:(i % 4 + 1) * 128],
                                        self.A[:, kc, i * 128:(i + 1) * 128], wt[:, kc, :],
                                        first[i // 4], (i % 4 == 3 and kc == 15),
                                        R=[wtb, self.Ab[kc]], W=[self.pb[bnk]], sgc=True)
                                first[i // 4] = False
                        self.act(vst[:, :, :].rearrange("p a b -> p (a b)"), self.G(grp), AF.Copy, R=self.Gb(grp),
                                 W=[vstb])
                        slot = idx + 3
                        self.sdma(self.at[hk, slot], vst[:, :, :].rearrange("p a b -> p (a b)"), R=[vstb],
                                  W=[self.atb[hk][slot]], ds=vds)
                    grp ^= 1
                    self.fill(1)
            self.fill(100)

    def phase2b(self, l, hk):
        nc = self.nc
        with ExitStack() as es:
            A = self.A
            qT = A[:, 0:3, :]
            qrT = A[:, 3:6, :]
            kcmp = A[:, 6, :]
            vcmp = A[:, 7, :]
            ksT = A[:, 8, :]
            kwT = A[:, 9, :]
            oT = A[:, 12:15, :]
            lds = None
            inb = Buf("attn_in")
            for slot in range(10):
                self.sdma(A[:, slot, :], self.at[hk, slot], R=[self.atb[hk][slot]], W=[inb], ds=lds)
            vs = self.sb(es, "vs", [128, 16, 132], BF16)
            vw = self.sb(es, "vw", [128, 16, 132], BF16)
            vb_ = Buf("vsvw")
            self.op(self.DVE, lambda: nc.vector.memset(vs[:, :, 128:129], 1.0), R=(), W=[vb_])
            self.op(self.DVE, lambda: nc.vector.memset(vw[:, :, 128:129], 1.0), R=(), W=[vb_])
            self.sdma(vs[:, :, 0:128], self.at[hk, 10].rearrange("p (a b) -> p a b", b=128), R=[self.atb[hk][10]],
                      W=[vb_], ds=lds)
            self.sdma(vw[:, :, 0:128], self.at[hk, 11].rearrange("p (a b) -> p a b", b=128), R=[self.atb[hk][11]],
                      W=[vb_], ds=lds)
            maskC = self.sb(es, "maskC", [128, 2048], BF16)
            Eall = self.sb(es, "Eall", [128, 2048], BF16)
            cmk = self.sb(es, "cmk", [128, 2, 128], BF16)
            selM = self.sb(es, "selM", [128, 512], F32)
            cb2 = Buf("cb2")
            self.sdma(maskC[:, :], self.cb_d[:, 0:2048], R=(), W=[cb2], ds=lds)
            self.sdma(Eall[:, :], self.cb_d[:, 2048:4096], R=(), W=[cb2], ds=lds)
            self.sdma(cmk[:, :, :].rearrange("p a b -> p (a b)"), self.cb_d[:, 4096:4096 + 256], R=(), W=[cb2], ds=lds)
            self.sdma(selM[:, :], self.cf_d[:, CF_OFF["mulM"]:CF_OFF["mulM"] + 512], R=(), W=[cb2], ds=lds)
            vcx = self.sb(es, "vcx", [128, 164], BF16)
            vcxb = Buf("vcx")
            self.op(self.DVE, lambda: nc.vector.memset(vcx[:, :], 0.0), R=(), W=[vcxb])
            self.op(self.DVE, lambda: nc.vector.memset(vcx[:, 128:129], 1.0), R=(), W=[vcxb])
            self.sdma(vcx[:, 129:161], self.cb_d[:, CB_OFF["OV"]:CB_OFF["OV"] + 32], R=(), W=[vcxb], ds=lds)
            kcT = self.sb(es, "kcT", [128, 128], BF16)
            kcTb = Buf("kcT")
            slo = self.sb(es, "slo", [128, 2048], BF16)
            shi = self.sb(es, "shi", [128, 2048], BF16)
            slob = Buf("slohi")
            w2t, w2tb = self.wnext(("w2", l))
            self.op(self.DVE, lambda: nc.vector.tensor_copy(out=self.w2sb[:, :, :], in_=w2t[:, 0:2, :]), R=[w2tb],
                    W=[self.w2sbb])
            xk = self.sb(es, "xk", [128, 128], F32)
            xkb = Buf("xk")
            xt = self.sb(es, "xkt", [128, 128], F32)
            xtb = Buf("xkt")
            gT = self.sb(es, "gT", [128, 128], BF16)
            gTb = Buf("gT")
            for which in range(2):
                src = kcmp if which == 0 else vcmp
                srcv = src.rearrange("p (n r) -> p n r", r=16)
                lov = slo[:, :].rearrange("p (n r) -> p n r", r=16)
                hiv = shi[:, :].rearrange("p (n r) -> p n r", r=16)
                for rr in range(16):
                    self.ts(lov[:, :, rr], srcv[:, :, rr], self.P(l, "peT", rr), None, ALU.add, None,
                            R=[inb, self.ppb], W=[slob])
                    self.ts(hiv[:, :, rr], srcv[:, :, rr], self.P(l, "peT", 16 + rr), None, ALU.add, None,
                            R=[inb, self.ppb], W=[slob])
                w1n = "cmp_k_w1" if which == 0 else "cmp_v_w1"
                w1a, w1ab = self.wnext(("w1", w1n, l, 0))
                w1b, w1bb = self.wnext(("w1", w1n, l, 1))
                for li in range(32):
                    wt_, wtb_ = (w1a, w1ab) if li < 16 else (w1b, w1bb)
                    n0, rr = li // 16, li % 16
                    sv = lov if li < 16 else hiv
                    self.mm(self.ps[:, 0, 0:127], wt_[:, li % 16, :], sv[:, n0:n0 + 127, rr], li == 0, li == 31,
                            R=[wtb_, slob], W=[self.pb[0]])
                self.act(xk[:, 0:127], self.ps[:, 0, 0:127], AF.Copy, R=[self.pb[0]], W=[xkb])
                self.gelu(gT[:, 0:127], xk[:, 0:127], xt[:, 0:127], R=[xkb], Wb=[gTb], tmpb=xtb)
                if which == 0:
                    self.mm(self.ps[:, 2, 0:127], self.w2sb[:, 0, :], gT[:, 0:127], True, True, R=[self.w2sbb, gTb],
                            W=[self.pb[2]])
                    self.act(kcT[:, 0:127], self.ps[:, 2, 0:127], AF.Copy, R=[self.pb[2]], W=[kcTb])
                else:
                    self.mm(self.ps[0:127, 2, 0:128], gT[:, 0:127], self.w2sb[:, 1, :], True, True,
                            R=[self.w2sbb, gTb], W=[self.pb[2]])
                    self.act(vcx[0:127, 0:128], self.ps[0:127, 2, 0:128], AF.Copy, R=[self.pb[2]], W=[vcxb])
            esb = [self.sb(es, f"esb{i}", [128, 3, 128], BF16) for i in range(5)]
            esbb = [Buf(f"esb{i}") for i in range(5)]
            ectr = [0]
            den = self.sb(es, "den", [128, 3, 3], F32)
            denb = Buf("den")
            coef = self.sb(es, "coef", [128, 3, 3], F32)
            coefb = Buf("coef")
            imp = self.sb(es, "imp", [128, 32], F32)
            impb = Buf("imp")
            sc2 = self.sb(es, "sc2", [128, 32], F32)
            sc2b = Buf("sc2")
            m8 = self.sb(es, "m8", [128, 16], F32)
            m8b = Buf("m8")
            selb_ = self.sb(es, "selbf", [128, 128], BF16)
            selbb = Buf("selbf")
            self.op(self.DVE, lambda: nc.vector.memset(selb_[:, :], 0.0), R=(), W=[selbb])
            selneg = self.sb(es, "selneg", [128, 3, 128], BF16)
            selnegb = Buf("selneg")
            self.op(self.DVE, lambda: nc.vector.memset(selneg[:, :, :], 0.0), R=(), W=[selnegb])
            ofin = [self.sb(es, f"ofin{k}", [128, 3, 128], F32) for k in range(2)]
            ofinb = [Buf(f"ofin{k}") for k in range(2)]
            obf = [self.sb(es, f"obf{k}", [128, 3, 128], BF16) for k in range(2)]
            obfb = [Buf(f"obf{k}") for k in range(2)]
            oTb = Buf("oT")
            rd = self.sb(es, "rdc", [128, 3], F32)
            rdb = Buf("rdc")
            PS_S = [0, 1, 6, 7]
            PS_T, PS_OC, PS_OS, PS_OW = 2, 3, 4, 5
            sctr = [0]
            psb16 = self.ps[:, PS_T, :].bitcast(BF16)
            DEPTH_Q = 3
            fifo = []
            delayed = []

            def pop_one():
                pv, after = fifo.pop(0)
                pv()
                if after is not None:
                    after()
                for t in delayed:
                    t[0] -= 1
                for t in [t for t in delayed if t[0] <= 0]:
                    delayed.remove(t)
                    t[1]()

            def push(pv, after):
                fifo.append((pv, after))
                while len(fifo) > DEPTH_Q:
                    pop_one()

            def drain():
                while fifo:
                    pop_one()
                while delayed:
                    delayed.pop(0)[1]()

            def pair(i, kT_ap, nk, q_src, v_ap, vcols, ps_o, mk, use_sel, first, last, after=None):
                qsl = slice(i * 128, (i + 1) * 128)
                sb_ = PS_S[sctr[0] % 4]
                sctr[0] += 1
                self.mm(self.ps[0:nk, sb_, 0:384], kT_ap, q_src[:, :, qsl], True, use_sel is None,
                        R=[inb, kcTb], W=[self.pb[sb_]], signal=use_sel is None)
                if use_sel is not None:
                    self.mm(self.ps[0:nk, sb_, 0:384], use_sel, selneg[:, :, :], False, True,
                            R=[cb2, selnegb], W=[self.pb[sb_]])
                ei = ectr[0] % 5
                ectr[0] += 1
                e, eb = esb[ei], esbb[ei]
                self.act(e[0:nk, :, :].rearrange("p a b -> p (a b)"), self.ps[0:nk, sb_, 0:384], AF.Exp,
                         R=[self.pb[sb_]], W=[eb], scale=SCALE)
                if mk is not None:
                    for g in range(3):
                        self.tt(e[0:nk, g, :], e[0:nk, g, :], mk, ALU.mult, R=[eb, cb2], W=[eb])

                def pv():
                    for g in range(3):
                        lst = last and g == 2
                        self.mm(self.ps[:, ps_o, g * vcols:(g + 1) * vcols], e[0:nk, g, :], v_ap,
                                first and g == 0, lst, R=[eb, vb_, vcxb], W=[self.pb[ps_o]], signal=lst, sgc=True)
                push(pv, after)

            epi_done = [False] * 16

            def make_epilogue(i, par, sel_active):
                def epilogue():
                    self.ts(den[:, :, 0], self.ps[:, PS_OC, 0:483].rearrange("p (g c) -> p g c", c=161)[:, :, 128],
                            1e-30, None, ALU.max, None, R=[self.pb[PS_OC]], W=[denb])
                    self.op(self.DVE, lambda: nc.vector.reciprocal(out=rd[:, :], in_=den[:, :, 0]), R=[denb], W=[rdb])
                    self.tt(coef[:, :, 0], rd[:, :], self.gsb[:, i, hk * 9:hk * 9 + 9].rearrange(
                        "p (g c) -> p g c", c=3)[:, :, 0], ALU.mult, R=[rdb, self.gsbb], W=[coefb])
                    for g in range(3):
                        self.ts(ofin[par][:, g, :], self.ps[:, PS_OC, g * 161:g * 161 + 128], coef[:, g, 0:1], None,
                                ALU.mult, None, R=[self.pb[PS_OC], coefb], W=[ofinb[par]])
                    if sel_active:
                        for g in range(3):
                            src_ = self.ps[:, PS_OC, g * 161 + 129:g * 161 + 161]
                            if g == 0:
                                self.ts(imp[:, :], src_, rd[:, 0:1], None, ALU.mult, None,
                                        R=[self.pb[PS_OC], rdb], W=[impb])
                            else:
                                self.stt(imp[:, :], src_, rd[:, g:g + 1], imp[:, :], ALU.mult, ALU.add,
                                         R=[self.pb[PS_OC], rdb, impb], W=[impb])
                        mo = (i - 8) * 32
                        self.tt(imp[:, :], imp[:, :], selM[:, mo:mo + 32], ALU.mult, R=[impb, cb2], W=[impb])
                        self.tt(imp[:, :], imp[:, :], selM[:, 256 + mo:256 + mo + 32], ALU.add, R=[impb, cb2], W=[impb])
                        self.op(self.DVE, lambda: nc.vector.max(out=m8[:, 0:8], in_=imp[:, :]), R=[impb], W=[m8b])
                        self.op(self.DVE, lambda: nc.vector.match_replace(out=sc2[:, :], in_to_replace=m8[:, 0:8],
                                                                          in_values=imp[:, :], imm_value=-3.0e38),
                                R=[impb, m8b], W=[sc2b])
                        self.op(self.DVE, lambda: nc.vector.max(out=m8[:, 8:16], in_=sc2[:, :]), R=[sc2b, m8b], W=[m8b])
                        self.ts(sc2[:, :], imp[:, :], m8[:, 15:16], None, ALU.is_ge, None, R=[impb, m8b, sc2b], W=[sc2b])
                        self.ts(selb_[:, 0:32], sc2[:, :], -1.0, MASKNEG, ALU.add, ALU.mult, R=[sc2b], W=[selbb])
                    epi_done[i] = True
                return epilogue

            def make_finalize(i, par):
                qsl = slice(i * 128, (i + 1) * 128)

                def finalize():
                    self.ts(den[:, :, 1], self.ps[:, PS_OS, 0:387].rearrange("p (g c) -> p g c", c=129)[:, :, 128],
                            1e-30, None, ALU.max, None, R=[self.pb[PS_OS]], W=[denb])
                    self.ts(den[:, :, 2], self.ps[:, PS_OW, 0:387].rearrange("p (g c) -> p g c", c=129)[:, :, 128],
                            1e-30, None, ALU.max, None, R=[self.pb[PS_OW]], W=[denb])
                    gv = self.gsb[:, i, hk * 9:hk * 9 + 9].rearrange("p (g c) -> p g c", c=3)
                    self.op(self.DVE, lambda: nc.vector.reciprocal(out=coef[:, :, 1:3], in_=den[:, :, 1:3]),
                            R=[denb, coefb], W=[coefb])
                    self.tt(coef[:, :, 1:3], coef[:, :, 1:3], gv[:, :, 1:3], ALU.mult, R=[coefb, self.gsbb], W=[coefb])
                    for g in range(3):
                        self.stt(ofin[par][:, g, :], self.ps[:, PS_OS, g * 129:g * 129 + 128], coef[:, g, 1:2],
                                 ofin[par][:, g, :], ALU.mult, ALU.add, R=[self.pb[PS_OS], coefb, ofinb[par]],
                                 W=[ofinb[par]])
                        self.stt(obf[par][:, g, :], self.ps[:, PS_OW, g * 129:g * 129 + 128], coef[:, g, 2:3],
                                 ofin[par][:, g, :], ALU.mult, ALU.add, R=[self.pb[PS_OW], coefb, ofinb[par]],
                                 W=[obfb[par]])

                    def do_T():
                        for g in range(3):
                            self.op(self.PE, lambda g=g: nc.tensor.transpose(psb16[:, 128 + g * 128:256 + g * 128],
                                                                             obf[par][:, g, :], self.ident[:, :]),
                                    R=[obfb[par], self.cstb], W=[self.pb[PS_T]])
                        for g in range(3):
                            self.act(oT[:, g, qsl], psb16[:, 128 + g * 128:256 + g * 128], AF.Copy,
                                     R=[self.pb[PS_T]], W=[oTb])
                    delayed.append([2, do_T])
                return finalize

            for i in range(16):
                qsl = slice(i * 128, (i + 1) * 128)
                par = i % 2
                sel_active = i >= 8
                pair(i, kcT[:, 0:127], 127, qT, vcx[0:127, 0:161], 161, PS_OC, maskC[0:127, qsl], None, True, True,
                     after=make_epilogue(i, par, sel_active))
                wk = list(range(max(0, i - 4), i + 1))
                for ki, kt in enumerate(wk):
                    mk = cmk[:, 0, :] if kt == i else (cmk[:, 1, :] if kt == i - 4 else None)
                    pair(i, kwT[:, kt * 128:(kt + 1) * 128], 128, qrT, vw[:, kt, 0:129], 129, PS_OW, mk, None,
                         ki == 0, ki == len(wk) - 1)
                if sel_active:
                    if not epi_done[i]:
                        drain()
                    self.op(self.PE, lambda: nc.tensor.transpose(psb16[:, 0:128], selb_[:, :], self.ident[:, :]),
                            R=[selbb, self.cstb], W=[self.pb[PS_T]])
                    for g in range(3):
                        self.op(self.DVE, lambda g=g: nc.vector.tensor_copy(out=selneg[:, g, :], in_=psb16[:, 0:128]),
                                R=[self.pb[PS_T]], W=[selnegb])
                for kt in range(i + 1):
                    pair(i, ksT[:, kt * 128:(kt + 1) * 128], 128, qrT, vs[:, kt, 0:129], 129, PS_OS,
                         cmk[:, 0, :] if kt == i else None,
                         Eall[:, kt * 128:(kt + 1) * 128] if sel_active else None, kt == 0, kt == i,
                         after=(make_finalize(i, par) if kt == i else None))
            drain()
            for g in range(3):
                self.sdma(self.os_[3 * hk + g], oT[:, g, :], R=[oTb], W=[self.osb[3 * hk + g]], ds=lds)

    def phase3(self, l):
        nc = self.nc
        with ExitStack() as es:
            A = self.A
            inb = Buf("p3in")
            for c in range(4):
                self.sdma(A[:, c, :], self.us[c], R=[self.usb[c]], W=[inb], ds=None)
            for c in range(6):
                self.sdma(A[:, 4 + c, :], self.rbs[c], R=[self.rbsb[c]], W=[inb], ds=None)
                self.sdma(A[:, 10 + c, :], self.os_[c], R=[self.osb[c]], W=[inb], ds=None)
            sgt = [self.sb(es, f"sgt{k}", [128, 3, 2048], BF16) for k in range(2)]
            sgtb = [[Buf(f"sgt{k}_{p}") for p in range(3)] for k in range(2)]
            yt = self.acc[:, 0:512]
            ytb = Buf("yt")
            tq = [self.acc[:, 512:1024], self.acc[:, 1024:1536]]
            tqb = [Buf("tq0"), Buf("tq1")]
            bctr = 0
            qctr = 0
            blocks = [(0, 4), (4, 6), (10, 6)]

            def load_sg(j):
                k = j % 2
                for p in range(3):
                    self.sdma(sgt[k][:, p, :], self.sgs[p, j], R=[self.sgsb[p][j]], W=[sgtb[k][p]], ds=self.sgds[k][p])

            load_sg(0)
            for j in range(16):
                if j + 1 < 16:
                    load_sg(j + 1)
                k = j % 2
                w3, w3b = self.wnext(("m3", l, j))
                bs, bsb, ds = self.bstage()
                for tt in range(4):
                    sl = slice(tt * 512, (tt + 1) * 512)
                    for p, (ko, nk) in enumerate(blocks):
                        bk = bctr % 8
                        bctr += 1
                        for kc in range(nk):
                            self.mm(self.ps[:, bk, :], w3[:, ko + kc, :], A[:, ko + kc, sl], kc == 0, kc == nk - 1,
                                    R=[w3b, inb], W=[self.pb[bk]])
                        if p == 0:
                            self.tt(yt, sgt[k][:, 0, sl], self.ps[:, bk, :], ALU.mult, R=[sgtb[k][0], self.pb[bk]],
                                    W=[ytb])
                        else:
                            q_, qb_ = tq[qctr % 2], tqb[qctr % 2]
                            qctr += 1
                            self.tt(q_, sgt[k][:, p, sl], self.ps[:, bk, :], ALU.mult, R=[sgtb[k][p], self.pb[bk]],
                                    W=[qb_])
                            if p == 1:
                                self.tt(yt, yt, q_, ALU.add, R=[qb_, ytb], W=[ytb])
                            else:
                                self.tt(bs[:, sl], yt, q_, ALU.add, R=[qb_, ytb], W=[bsb])
                self.sdma(self.ys[j], bs[:, :], R=[bsb], W=[self.ysb[j]], ds=ds)

    def resid_update(self, l, j, grp, sq, sqb, stats, first_stats, src_l0):
        st, stb, ds = self.fstage()
        if src_l0:
            src, srcb = self.xsrc(l, j)
        else:
            src, srcb = self.xs[j], self.xsb[j]
        self.sdma(st[:, :], src, R=[srcb], W=[stb], ds=ds)
        self.tt(st[:, :], st[:, :], self.G(grp), ALU.add, R=[stb] + self.Gb(grp), W=[stb])
        self.sdma(self.xs[j], st[:, :], R=[stb], W=[self.xsb[j]], ds=ds)
        if stats:
            self.acc_update(st, stb, sq, sqb, first_stats)

    def phase4(self, l):
        with ExitStack() as es:
            sq = self.sb(es, "p4sq", [128, 2048], F32)
            sqb = Buf("p4sq")
            for c in range(16):
                self.sdma(self.A[:, c, :], self.ys[c], R=[self.ysb[c]], W=[self.Ab[c]], ds=None)
            for j in range(16):
                wt, wtb = self.wnext(("col", "w_o", l, j * 128, 128))
                grp = j % 2
                self.proj_fm(wt, wtb, 16, self.hrhs, self.Ab, grp)
                self.resid_update(l, j, grp, sq, sqb, True, j == 0, True)

    def phase6(self, l):
        nc = self.nc
        with ExitStack() as es:
            actT = self.sb(es, "actT", [128, 16, 2048], BF16)
            actb = [Buf(f"act{f}") for f in range(16)]
            gctr = 0
            for qf in range(4):
                for f in range(16):
                    wt, wtb = self.wnext(("col", "w_mlp_up", l, (qf * 16 + f) * 128, 128))
                    grp = gctr % 2
                    gctr += 1
                    self.proj_fm(wt, wtb, 16, self.hrhs, self.Ab, grp)
                    st, stb, ds = self.fstage()
                    self.act(st[:, :], self.G(grp), AF.Square, R=self.Gb(grp), W=[stb])
                    self.stt(actT[:, f, :], self.G(grp), 0.0, st[:, :], ALU.is_gt, ALU.mult, R=self.Gb(grp) + [stb],
                             W=[actb[f]])
                for j in range(16):
                    wt, wtb = self.wnext(("dn", l, qf, j))
                    grp = gctr % 2
                    gctr += 1
                    self.proj_fm(wt, wtb, 16, lambda kc, tt: actT[:, kc, tt * 512:(tt + 1) * 512], actb, grp)
                    self.resid_update(l, j, grp, self.rstdB, self.rstdb, qf == 3, j == 0, False)

    def final(self):
        self.norm_from_acc()
        for c in range(16):
            st, stb, ds = self.fstage()
            self.sdma(st[:, :], self.xs[c], R=[self.xsb[c]], W=[stb], ds=ds)
            o = PP_OFF["g_final"] + c
            self.stt(st[:, :], st[:, :], self.pp[:, o:o + 1], self.rstdB[:, :], ALU.mult, ALU.mult,
                     R=[stb, self.rstdb, self.ppb], W=[stb])
            self.sdma(self.out_d[c], st[:, :], R=[stb], W=[self.outb[c]], ds=ds)


_CACHE = {}


N_WTILES = 516


def _get_builder():
    if "b" not in _CACHE:
        b = Builder(DEPTH, False, N_WTILES)
        b.build()
        assert len(b.worder) == N_WTILES, len(b.worder)
        _CACHE["b"] = b
    return _CACHE["b"]


def kernel(**inputs):
    inp = {k: np.asarray(v) for k, v in inputs.items()}
    bld = _get_builder()
    wpack = pack_weights(inp, bld.worder)
    pp = pack_params(inp)
    cf, cb = make_consts()
    x = inp["x"]
    nb = x.shape[0]
    in_maps = []
    for b in range(nb):
        xT = np.ascontiguousarray(x[b].T).reshape(16, 128, 2048)
        in_maps.append({"x_in": xT, "wpack": wpack, "pp": pp, "cf": cf, "cb": cb})
    nc = bld.nc
    res = run_bass_kernel_spmd(nc, in_maps, core_ids=list(range(nb)))
    outs = []
    for b in range(nb):
        o = np.asarray(res.results[b]["out"]).reshape(2048, 2048)
        outs.append(np.ascontiguousarray(o.T))
    return np.stack(outs, 0).astype(np.float32)
```
